# Optimizing a Trainium2 kernel written in Bass

```python
import jax, jax.numpy as jnp
from jax import lax
import numpy as np

D_MODEL = 2048
BATCH = 2
SEQ = 4096
DEPTH = 1

N_MEM = 256
ATT_HEAD_DIM = 128
ATT_WIDTH = D_MODEL // 2
ATT_HEADS = ATT_WIDTH // ATT_HEAD_DIM
MOBA_BLOCK = 256
MOBA_TOPK = 3
Q_CHUNK = 64
SGU_WIDTH = D_MODEL // 2
SGU_CHUNK = 128
SGU_GROUP_DIM = 128
SGU_GROUPS = SGU_WIDTH // SGU_GROUP_DIM
MEM_HEADS = 4
MEM_WIDTH = D_MODEL // 2
MEM_HEAD_DIM = MEM_WIDTH // MEM_HEADS
N_BRANCH = 3
DN_ALPHA = (2 * DEPTH) ** 0.25
DN_BETA = (8 * DEPTH) ** -0.25
LN_EPS = 1e-5
IN_WIDTHS = (ATT_WIDTH, ATT_WIDTH, ATT_WIDTH, ATT_WIDTH,
             SGU_WIDTH, SGU_WIDTH, SGU_WIDTH,
             MEM_WIDTH, MEM_WIDTH,
             N_BRANCH * D_MODEL)

kernel_name = 'moba_gmlp_memxattn_gated_hybrid'


def _layer_norm(x, g, b):
    xf = x.astype(jnp.float32)
    mu = xf.mean(-1, keepdims=True)
    var = jnp.square(xf - mu).mean(-1, keepdims=True)
    return ((xf - mu) * lax.rsqrt(var + LN_EPS) * g + b).astype(x.dtype)


def _moba_attention(q, k, v):
    b, s, h, dh = q.shape
    nb = -(-s // MOBA_BLOCK)
    n_sel = min(MOBA_TOPK, nb)
    pad = nb * MOBA_BLOCK - s
    q = q.transpose(0, 2, 1, 3)
    k = jnp.pad(k.transpose(0, 2, 1, 3), ((0, 0), (0, 0), (0, pad), (0, 0)))
    v = jnp.pad(v.transpose(0, 2, 1, 3), ((0, 0), (0, 0), (0, pad), (0, 0)))
    k_blocks = k.reshape(b, h, nb, MOBA_BLOCK, dh)
    v_blocks = v.reshape(b, h, nb, MOBA_BLOCK, dh)
    k_mean = k_blocks.mean(axis=3)
    scale = dh ** -0.5
    blk_ids = jnp.arange(nb)
    slot_ids = jnp.arange(n_sel)
    key_off = jnp.arange(MOBA_BLOCK)
    q_off = jnp.arange(Q_CHUNK)
    gather = jax.vmap(jax.vmap(lambda blocks, ids: blocks[ids]))

    def chunk(start):
        qc = lax.dynamic_slice_in_dim(q, start, Q_CHUNK, axis=2)
        blk = start // MOBA_BLOCK
        s_blk = jnp.einsum('bhqd,bhnd->bhqn', qc, k_mean)
        s_blk = jnp.where(blk_ids < blk, s_blk, -jnp.inf)
        _, sel = lax.top_k(s_blk, n_sel)
        k_sel = gather(k_blocks, sel)
        v_sel = gather(v_blocks, sel)
        s_sel = jnp.einsum('bhqd,bhqrkd->bhqrk', qc, k_sel) * scale
        s_sel = jnp.where((slot_ids < blk)[:, None], s_sel, -jnp.inf)
        k_own = lax.dynamic_slice_in_dim(k, blk * MOBA_BLOCK, MOBA_BLOCK, axis=2)
        v_own = lax.dynamic_slice_in_dim(v, blk * MOBA_BLOCK, MOBA_BLOCK, axis=2)
        s_own = jnp.einsum('bhqd,bhkd->bhqk', qc, k_own) * scale
        causal = (blk * MOBA_BLOCK + key_off)[None, :] <= (start + q_off)[:, None]
        s_own = jnp.where(causal, s_own, -jnp.inf)
        logits = jnp.concatenate([s_sel.reshape(b, h, Q_CHUNK, n_sel * MOBA_BLOCK), s_own], axis=-1)
        p = jax.nn.softmax(logits.astype(jnp.float32), axis=-1).astype(v.dtype)
        p_sel = p[..., :n_sel * MOBA_BLOCK].reshape(b, h, Q_CHUNK, n_sel, MOBA_BLOCK)
        p_own = p[..., n_sel * MOBA_BLOCK:]
        return (jnp.einsum('bhqrk,bhqrkd->bhqd', p_sel, v_sel)
                + jnp.einsum('bhqk,bhkd->bhqd', p_own, v_own))

    out = lax.map(chunk, jnp.arange(0, s, Q_CHUNK))
    return out.transpose(1, 0, 3, 2, 4).reshape(b, s, h * dh)


def _spatial_gating(u, v, w_s, b_s, ln_v_g, ln_v_b):
    b, s, _ = u.shape
    u = jax.nn.gelu(u)
    v = _layer_norm(jax.nn.gelu(v), ln_v_g, ln_v_b)
    vc = v.reshape(b, s // SGU_CHUNK, SGU_CHUNK, SGU_GROUPS, SGU_GROUP_DIM)
    tril = jnp.tril(jnp.ones((SGU_CHUNK, SGU_CHUNK), dtype=bool))
    w_causal = jnp.where(tril[None], w_s, 0.0)
    mixed = jnp.einsum('gts,bcsgd->bctgd', w_causal, vc) + b_s.T[None, None, :, :, None]
    return u * mixed.reshape(b, s, SGU_WIDTH)


def _memory_attention(q, mem_k, mem_v):
    b, s, h, dh = q.shape
    logits = jnp.einsum('bshd,bmhd->bhsm', q, mem_k) * (dh ** -0.5)
    p = jax.nn.softmax(logits.astype(jnp.float32), axis=-1).astype(mem_v.dtype)
    return jnp.einsum('bhsm,bmhd->bshd', p, mem_v).reshape(b, s, h * dh)


def setup_inputs(seed: int = 0) -> dict:
    key = jax.random.key(seed)
    ks = jax.random.split(key, 32)

    def nrm(k, shape, fan_in, scale=1.0):
        return jax.random.normal(k, shape, jnp.float32) * (scale * fan_in ** -0.5)

    in_scales = (1.0, 1.0, DN_BETA, 1.0, DN_BETA, DN_BETA, 1.0, 1.0, 1.0, 1.0)
    w_in = jnp.concatenate([nrm(ks[10 + i], (D_MODEL, w), D_MODEL, sc)
                            for i, (w, sc) in enumerate(zip(IN_WIDTHS, in_scales))], axis=1)
    return {
        'x': jax.random.normal(ks[0], (BATCH, SEQ, D_MODEL), jnp.float32),
        'mem': jax.random.normal(ks[1], (BATCH, N_MEM, D_MODEL), jnp.float32),
        'w_in': w_in,
        'w_mem_k': nrm(ks[2], (D_MODEL, MEM_WIDTH), D_MODEL),
        'w_mem_v': nrm(ks[3], (D_MODEL, MEM_WIDTH), D_MODEL, DN_BETA),
        'w_s': nrm(ks[4], (SGU_GROUPS, SGU_CHUNK, SGU_CHUNK), SGU_CHUNK),
        'b_s': 1.0 + 0.1 * jax.random.normal(ks[5], (SGU_GROUPS, SGU_CHUNK), jnp.float32),
        'ln_v_g': 1.0 + 0.02 * jax.random.normal(ks[6], (SGU_WIDTH,), jnp.float32),
        'ln_v_b': 0.02 * jax.random.normal(ks[7], (SGU_WIDTH,), jnp.float32),
        'w_branch_attn': nrm(ks[8], (ATT_WIDTH, D_MODEL), ATT_WIDTH, DN_BETA),
        'w_branch_sgu': nrm(ks[9], (SGU_WIDTH, D_MODEL), SGU_WIDTH, DN_BETA),
        'w_branch_mem': nrm(ks[20], (MEM_WIDTH, D_MODEL), MEM_WIDTH, DN_BETA),
        'w_out': nrm(ks[21], (D_MODEL, D_MODEL), D_MODEL, DN_BETA),
        'ln_g': 1.0 + 0.02 * jax.random.normal(ks[22], (D_MODEL,), jnp.float32),
        'ln_b': 0.02 * jax.random.normal(ks[23], (D_MODEL,), jnp.float32),
    }


def reference(x, mem, w_in, w_mem_k, w_mem_v, w_s, b_s, ln_v_g, ln_v_b,
              w_branch_attn, w_branch_sgu, w_branch_mem, w_out, ln_g, ln_b):
    b, s, d = x.shape
    m = mem.shape[1]
    split_at = np.cumsum(IN_WIDTHS)[:-1].tolist()
    for _ in range(DEPTH):
        proj = x @ w_in
        q_a, k_a, v_a, z_a, u_g, v_g, z_g, q_c, z_c, g_logit = jnp.split(proj, split_at, axis=-1)
        y_a = _moba_attention(q_a.reshape(b, s, ATT_HEADS, ATT_HEAD_DIM),
                              k_a.reshape(b, s, ATT_HEADS, ATT_HEAD_DIM),
                              v_a.reshape(b, s, ATT_HEADS, ATT_HEAD_DIM)) * jax.nn.silu(z_a)
        y_g = _spatial_gating(u_g, v_g, w_s, b_s, ln_v_g, ln_v_b) * jax.nn.silu(z_g)
        mem_k = (mem @ w_mem_k).reshape(b, m, MEM_HEADS, MEM_HEAD_DIM)
        mem_v = (mem @ w_mem_v).reshape(b, m, MEM_HEADS, MEM_HEAD_DIM)
        y_c = _memory_attention(q_c.reshape(b, s, MEM_HEADS, MEM_HEAD_DIM), mem_k, mem_v) * jax.nn.silu(z_c)
        gates = jax.nn.sigmoid(g_logit.astype(jnp.float32)).astype(x.dtype).reshape(b, s, N_BRANCH, d)
        merged = (gates[:, :, 0] * (y_a @ w_branch_attn)
                  + gates[:, :, 1] * (y_g @ w_branch_sgu)
                  + gates[:, :, 2] * (y_c @ w_branch_mem))
        y = merged @ w_out
        x = _layer_norm(DN_ALPHA * x + y, ln_g, ln_b)
    return x
```

```python
import concourse.bass as bass
import concourse.mybir as mybir

ENGS = ("pe", "act", "dve", "pool", "sp")


class Buf:
    __slots__ = ("name", "w", "r_eng", "r_dma")

    def __init__(self, name):
        self.name = name
        self.w = None
        self.r_eng = {}
        self.r_dma = []


class Op:
    __slots__ = ("eng", "fn", "is_dma", "key", "val", "mark", "idx", "deps", "waits", "raw")

    def __init__(self, eng, fn, is_dma, key):
        self.eng = eng
        self.fn = fn
        self.is_dma = is_dma
        self.key = key
        self.val = 0
        self.mark = False
        self.idx = -1
        self.deps = []
        self.waits = []
        self.raw = set()


class Tracker:
    def __init__(self, nc):
        self.nc = nc
        self.ops = {e: [] for e in ENGS}
        self.dma_cnt = {}
        self.nops = 0
        self.last_dma = {}

    def op(self, eng, fn, reads=(), writes=(), dma_key=None):
        o = Op(eng, fn, dma_key is not None, dma_key)
        deps = []
        for b in reads:
            if b.w is not None:
                deps.append(b.w)
                o.raw.add(id(b.w))
        for b in writes:
            if b.w is not None:
                deps.append(b.w)
            deps.extend(b.r_eng.values())
            deps.extend(b.r_dma)
        seen = set()
        for d in deps:
            if d is o or id(d) in seen:
                continue
            seen.add(id(d))
            if (not d.is_dma) and (not o.is_dma) and d.eng == eng:
                if eng == "pe":
                    continue
            o.deps.append(d)
        for b in reads:
            if o.is_dma:
                b.r_dma.append(o)
            else:
                b.r_eng[eng] = o
        for b in writes:
            b.w = o
            b.r_eng = {}
            b.r_dma = []
        if o.is_dma:
            c = self.dma_cnt.get(dma_key, 0) + 16
            self.dma_cnt[dma_key] = c
            o.val = c
            self.last_dma[dma_key] = o
        o.idx = len(self.ops[eng])
        self.ops[eng].append(o)
        self.nops += 1
        return o

    def dma(self, eng, out, in_, reads, writes, key, **kw):
        return self.op(eng, lambda e: e.dma_start(out=out, in_=in_, **kw), reads, writes, dma_key=key)

    def barrier(self, skip=("ws", "kt0", "vh0", "mg", "wq")):
        lasts = []
        for e in ENGS:
            for o in reversed(self.ops[e]):
                if not o.is_dma and o.fn is not None:
                    lasts.append(o)
                    break
        for k, o in self.last_dma.items():
            if any(str(k).startswith(p) for p in skip):
                continue
            lasts.append(o)
        for e in ENGS:
            b = Op(e, None, False, None)
            b.deps = list(lasts)
            b.idx = len(self.ops[e])
            self.ops[e].append(b)

    def finalize_marks(self):
        for e in ENGS:
            seen_idx = {}
            seen_dma = {}
            for o in self.ops[e]:
                for d in o.deps:
                    if d.is_dma:
                        if seen_dma.get(d.key, 0) < d.val:
                            seen_dma[d.key] = d.val
                            o.waits.append(d)
                    else:
                        if seen_idx.get(d.eng, -1) < d.idx:
                            seen_idx[d.eng] = d.idx
                            d.mark = True
                            o.waits.append(d)
        for e in ENGS:
            c = 0
            for o in self.ops[e]:
                if not o.is_dma and o.mark:
                    c += 1
                    o.val = c

    def emit(self, stack):
        nc = self.nc
        self.finalize_marks()
        esem = {e: stack.enter_context(nc.semaphore("s_" + e)) for e in ENGS}
        dsem = {k: stack.enter_context(nc.semaphore("d_%s" % (k,))) for k in self.dma_cnt}

        def run(e, h):
            for o in self.ops[e]:
                for d in o.waits:
                    if d.is_dma:
                        h.wait_ge(dsem[d.key], d.val)
                    else:
                        h.wait_ge(esem[d.eng], d.val)
                ins = o.fn(h) if o.fn is not None else None
                if ins is None:
                    continue
                if o.is_dma:
                    ins.then_inc(dsem[o.key], 16)
                elif o.mark:
                    ins.then_inc(esem[e], 1)

        with nc.Block() as block:
            @block.sync
            def _(h):
                run("sp", h)

            @block.scalar
            def _(h):
                run("act", h)

            @block.vector
            def _(h):
                run("dve", h)

            @block.gpsimd
            def _(h):
                run("pool", h)

            @block.tensor
            def _(h):
                run("pe", h)

import numpy as np
from contextlib import ExitStack
from concourse.bass_utils import run_bass_kernel_spmd

F32 = mybir.dt.float32
BF16 = mybir.dt.bfloat16
AF = mybir.ActivationFunctionType
ALU = mybir.AluOpType
AX = mybir.AxisListType

NEG = -30000.0
BIGM = 1.0e4
SCALE_A = 128.0 ** -0.5
SCALE_C = 256.0 ** -0.5
DN_ALPHA = 2.0 ** 0.25
LN_EPS = 1e-5


def slot_positions(s):
    return list(range(s)) + list(range(4, 4 + 3 * (s + 1)))


class _Stop(Exception):
    pass


def build_program(stop=None):
    nc = bass.Bass("TRN2", target_bir_lowering=False)
    di = lambda name, shape: nc.dram_tensor(name, shape, F32, kind="ExternalInput").ap()
    xTl = di("xTl", [16, 128, 16, 256])
    xown = di("xown", [8, 128, 2048])
    past_d = di("past", [128, 128])
    memTl = di("memTl", [128, 16, 256])
    wA = di("wA", [120, 128, 16, 128])
    wBva = di("wBva", [2, 128, 16, 512])
    wBvg = di("wBvg", [2, 128, 16, 512])
    wmk = di("wmk", [8, 128, 16, 128])
    wmv = di("wmv", [2, 128, 16, 512])
    wbr = di("wbr", [3, 16, 128, 8, 128])
    wo = di("wo", [4, 128, 16, 512])
    wsT_d = di("wsT", [128, 8, 128])
    tril_d = di("tril", [128, 128])
    bs_d = di("bs", [1, 1024])
    lnvg_d = di("lnvg", [1, 1024])
    lnvb_d = di("lnvb", [1, 1024])
    lng_d = di("lng", [1, 2048])
    lnb_d = di("lnb", [1, 2048])
    ident_d = di("ident", [128, 128])
    esel_d = di("esel", [128, 2048])
    cb_d = di("cb", [128, 2, 256])
    out_d = nc.dram_tensor("out", [8, 128, 2048], F32, kind="ExternalOutput").ap()
    kT_d = nc.dram_tensor("kT_scr", [8, 128, 4096], BF16).ap()
    v_d = nc.dram_tensor("v_scr", [8, 32, 128, 128], BF16).ap()

    T = Tracker(nc)
    with ExitStack() as st:
        sbt = lambda name, shape, dt: st.enter_context(nc.sbuf_tensor(name, shape, dt))
        R_XT = sbt("R_XT", [128, 16384], BF16)
        R_Y = sbt("R_Y", [128, 24576], BF16)
        R_MG = sbt("R_MG", [128, 16384], BF16)
        R_LW = sbt("R_LW", [128, 32768], BF16)
        identb = sbt("identb", [128, 128], BF16)
        onesb = sbt("onesb", [128, 128], BF16)
        eselb = sbt("eselb", [128, 2048], BF16)
        cbb = sbt("cbb", [128, 2, 256], BF16)
        past_s = sbt("past_s", [128, 128], F32)
        pm1b = sbt("pm1b", [128, 128], F32)
        kms = sbt("kms", [128, 8, 16], F32)
        kmb = sbt("kmb", [128, 8, 16], BF16)
        wcb = sbt("wcb", [128, 8, 128], BF16)
        PS = [st.enter_context(nc.psum_tensor("ps%d" % i, [128, 512], F32)) for i in range(7)]
        PT = st.enter_context(nc.psum_tensor("pst", [128, 1024], BF16))
        Pb = [Buf("ps%d" % i) for i in range(7)]
        PTb = Buf("pst")

        class Rot:
            def __init__(self, idxs):
                self.idxs = idxs
                self.i = 0

            def next(self):
                k = self.idxs[self.i % len(self.idxs)]
                self.i += 1
                return PS[k], Pb[k]

        evac_i = [0]

        def evac(out, in_, reads, writes, func=None, scale=1.0, eng=None):
            if func is not None:
                T.op("act", lambda e: e.activation(out=out, in_=in_, func=func, scale=scale), reads, writes)
                return
            if eng is None:
                eng = "act" if evac_i[0] % 2 == 0 else "dve"
                evac_i[0] += 1
            if eng == "act":
                T.op("act", lambda e: e.copy(out=out, in_=in_), reads, writes)
            else:
                T.op("dve", lambda e: e.tensor_copy(out=out, in_=in_), reads, writes)

        def mm(out, lhsT, rhs, start, stop, reads, writes):
            T.op("pe", lambda e: e.matmul(out, lhsT=lhsT, rhs=rhs, start=start, stop=stop), reads, writes)

        xT_own = R_XT[:, :].rearrange("p (k t) -> p k t", k=16)
        XTb = [Buf("xt%d" % i) for i in range(4)]
        wk_sb = R_Y[:, 0:16384].rearrange("p (c k n) -> p c k n", c=8, k=16)
        WKb = [Buf("wk%d" % i) for i in range(8)]
        Yv = R_Y[:, :].rearrange("p (c t) -> p c t", c=24)
        Yb = [Buf("y%d" % i) for i in range(24)]
        mgB = R_MG[:, :].rearrange("p (h k n) -> p h k n", h=2, k=16)
        MGb = [Buf("mg%d" % i) for i in range(2)]
        merged = R_MG[:, :].rearrange("p (c t) -> p c t", c=16)
        MRb = [Buf("mr%d" % i) for i in range(16)]
        ws = [R_LW[:, 24576 + i * 2048: 24576 + (i + 1) * 2048].rearrange("p (k n) -> p k n", k=16) for i in range(4)]
        WSb = [Buf("ws%d" % i) for i in range(4)]
        ws_i = [0]

        prefetched = {}

        def prefetch(key, src):
            prefetched[key] = stream(src)

        def stream(src, key=None):
            if key is not None and key in prefetched:
                return prefetched.pop(key)
            slot = ws_i[0] % 4
            ws_i[0] += 1
            T.dma("pool", ws[slot], src, [], [WSb[slot]], "ws%d" % slot)
            return ws[slot], WSb[slot]

        def L(off, n, dt=BF16):
            if dt == BF16:
                return R_LW[:, off // 2: off // 2 + n]
            return R_LW[:, off // 2: off // 2 + 2 * n].bitcast(F32)

        cB = Buf("consts")

        def chk(name):
            if stop == name:
                raise _Stop()
        def _phases():
            import os
            _d0 = os.environ.get("DBG0", "iecpmt")
            if "i" in _d0: T.dma("pool", identb[:], ident_d, [], [cB], "c0")
            if "e" in _d0: T.dma("pool", eselb[:], esel_d, [], [cB], "c0")
            if "c" in _d0: T.dma("pool", cbb[:], cb_d, [], [cB], "c0")
            if "p" in _d0: T.dma("sp", past_s[:], past_d, [], [cB], "c1")
            if "t" in _d0: T.op("dve", lambda e: e.tensor_scalar(out=pm1b[:], in0=past_s[:], scalar1=-1.0, scalar2=BIGM, op0=ALU.add, op1=ALU.mult), [cB], [cB])
            if "m" in _d0: T.op("dve", lambda e: e.memset(onesb[:], 1.0), [], [cB])
            chk("0")
            rotA = Rot([0, 1, 2, 3, 4, 5, 6])
            _da = os.environ.get("DBGA", "kvx")
            T.dma("pool", xT_own[:, :, 0:256], xTl[0], [], [XTb[0]], "xt0")
            for cg in range(8):
                if "k" in _da: T.dma("pool", wk_sb[:, cg], wA[8 + cg], [], [WKb[cg]], "wk%d" % cg)
            for half in range(2):
                for kh in range(2):
                    if "v" in _da: T.dma("pool", mgB[:, half, kh * 8:(kh + 1) * 8, :], wBva[half][:, kh * 8:(kh + 1) * 8, :], [], [MGb[half]], "mg%d" % half)
            chk("AW")
            if os.environ.get("DBGB"):
                T.barrier()
            xr = [L(i * 8192, 4096).rearrange("p (k t) -> p k t", k=16) for i in range(2)]
            XRb = [Buf("xr%d" % i) for i in range(2)]
            kst = [L(16384 + i * 4096, 2048).rearrange("p (h t) -> p h t", h=8) for i in range(2)]
            KSb = [Buf("kst%d" % i) for i in range(2)]
            vst = [L(24576 + i * 4096, 2048).rearrange("p (t c) -> p t c", t=2) for i in range(2)]
            VSb = [Buf("vst%d" % i) for i in range(2)]
            kmsB = Buf("kms")
            kT_v = kT_d.rearrange("h d t -> d h t")
            v_v = v_d.rearrange("h n p d -> p n h d")
            kTdB = Buf("kTd")
            vdB = Buf("vd")
            for ch in range(16):
                chk("A%d" % ch)
                if ch < 4:
                    xc, xcB = xT_own[:, :, ch * 256:(ch + 1) * 256], XTb[ch]
                    if ch > 0:
                        T.dma("pool", xc, xTl[ch], [], [xcB], "xt%d" % ch)
                else:
                    xc, xcB = xr[ch % 2], XRb[ch % 2]
                    T.dma("pool", xc, xTl[ch], [], [xcB], "xr%d" % (ch % 2))
                sl = ch % 2
                for h in range(8):
                    ps, pb = rotA.next()
                    for kc in range(16):
                        mm(ps[:, 0:256], wk_sb[:, h, kc, :], xc[:, kc, :], kc == 0, kc == 15, [WKb[h], xcB], [pb])
                    _dc = os.environ.get("DBGC", "er")
                    if "e" in _dc: evac(kst[sl][:, h, :], ps[:, 0:256], [pb], [KSb[sl]])
                    if "E" in _dc: evac(kst[sl][:, h, :], ps[:, 0:256], [pb], [KSb[sl]], eng="act")
                    if "r" in _dc: T.op("dve", lambda e, h=h, ch=ch, ps=ps: e.reduce_sum(out=kms[:, h, ch:ch + 1], in_=ps[:, 0:256], axis=AX.X), [pb], [kmsB, pb])
                    chk("AK%d" % h)
                chk("AK")
                T.dma("sp", kT_v[:, :, ch * 256:(ch + 1) * 256], kst[sl], [KSb[sl]], [kTdB], "kst%d" % sl)
                chk("AKD")
                for tt in range(2):
                    for half in range(2):
                        ps, pb = rotA.next()
                        for kc in range(16):
                            mm(ps[:, :], xc[:, kc, tt * 128:(tt + 1) * 128], mgB[:, half, kc, :], kc == 0, kc == 15, [MGb[half], xcB], [pb])
                        evac(vst[sl][:, tt, half * 512:(half + 1) * 512], ps[:, :], [pb], [VSb[sl]])
                chk("AV")
                for tt in range(2):
                    T.dma("sp", v_v[:, ch * 2 + tt], vst[sl][:, tt, :].rearrange("p (h d) -> p h d", h=8), [VSb[sl]], [vdB], "vst%d" % sl)
            T.op("dve", lambda e: e.tensor_scalar(out=kmb[:], in0=kms[:], scalar1=1.0 / 256.0, scalar2=None, op0=ALU.mult), [kmsB], [kmsB])
            prefetch("q0", wA[0])
            prefetch("z0", wA[24])
            prefetch("q1", wA[1])
            prefetch("z1", wA[25])
            kt0 = R_LW[:, 16384:20480]
            vh0 = R_LW[:, 20480:24576].rearrange("p (n d) -> p n d", n=32)
            KT0b, VH0b = Buf("kt0"), Buf("vh0")
            T.dma("sp", kt0, kT_d[0], [kTdB], [KT0b], "kt0")
            T.dma("sp", vh0, v_d[0].rearrange("n p d -> p n d"), [vdB], [VH0b], "vh0")
            T.barrier()
            chk("A")

            kt_sb = [L(32768, 4096), L(0, 4096)]
            KTb = [KT0b, Buf("kt1")]
            vh_sb = [L(40960, 4096).rearrange("p (n d) -> p n d", n=32), L(8192, 4096).rearrange("p (n d) -> p n d", n=32)]
            VHb = [VH0b, Buf("vh1")]
            qT = [L(16384 + i * 2048, 1024) for i in range(2)]
            QTb = [Buf("qT%d" % i) for i in range(2)]
            sz = [L(20480 + i * 2048, 1024) for i in range(2)]
            SZb = [Buf("sz%d" % i) for i in range(2)]
            pT = [L(24576 + i * 1024, 512) for i in range(4)]
            PTTb = [Buf("pT%d" % i) for i in range(4)]
            selbT2 = [L(28672 + i * 2048, 1024) for i in range(2)]
            SBT2b = [Buf("selbT%d" % i) for i in range(2)]
            for i in range(2):
                T.op("dve", lambda e, i=i: e.memset(selbT2[i], 0.0), [], [SBT2b[i]])
            sm = sbt("sm", [128, 128], F32)[:]
            sel = sbt("sel", [128, 128], F32)[:]
            top8 = sbt("top8", [128, 64], F32)[:]
            selb2 = [sbt("selb%d" % i, [128, 128], BF16)[:] for i in range(2)]
            selb2B = [Buf("selb%d" % i) for i in range(2)]
            rden = sbt("rden", [128, 512], F32)
            otmp = sbt("otmp", [128, 512], F32)
            pi = [0]
            smB, selB, top8B, selbB, rdenB, otmpB = [Buf(n) for n in "sm sel top8 selb rden otmp".split()]
            rotS = Rot([4, 5, 6])
            rotP = rotS
            accO, accOb = [PS[0], PS[1]], [Pb[0], Pb[1]]
            accD, accDb = [PS[2], PS[3]], [Pb[2], Pb[3]]

            def proj_fm(cg_src, dst, dstB, func=None, rot=None, key=None):
                w, wB = stream(cg_src, key)
                for half in range(2):
                    ps, pb = rot.next()
                    for kc in range(16):
                        mm(ps[:, :], w[:, kc, :], xT_own[:, kc, half * 512:(half + 1) * 512], kc == 0, kc == 15,
                           [wB, XTb[2 * half], XTb[2 * half + 1]], [pb])
                    evac(dst[:, half * 512:(half + 1) * 512], ps[:, :], [pb], [dstB], func=func)

            def proj_head_a(h):
                if h > 0:
                    T.dma("sp", kt_sb[h % 2], kT_d[h], [kTdB], [KTb[h % 2]], "kt%d" % (h % 2))
                    T.dma("sp", vh_sb[h % 2], v_d[h].rearrange("n p d -> p n d"), [vdB], [VHb[h % 2]], "vh%d" % (h % 2))
                proj_fm(wA[h], qT[h % 2], QTb[h % 2], rot=rotP, key="q%d" % h)
                proj_fm(wA[24 + h], sz[h % 2], SZb[h % 2], func=AF.Silu, rot=rotP, key="z%d" % h)
                q, qB = qT[h % 2], QTb[h % 2]
                selbT, SBTb = selbT2[h % 2], SBT2b[h % 2]
                psS, psSb = rotS.next()
                for tt in range(8):
                    mm(psS[:, tt * 16:(tt + 1) * 16], q[:, tt * 128:(tt + 1) * 128], kmb[:, h, :], True, True, [qB, kmsB], [psSb])
                T.op("dve", lambda e: e.tensor_tensor(out=sm, in0=psS[:, 0:128], in1=past_s[:], op=ALU.mult), [psSb, cB], [smB])
                T.op("dve", lambda e: e.tensor_tensor(out=sm, in0=sm, in1=pm1b[:], op=ALU.add), [smB, cB], [smB])
                for tt in range(8):
                    T.op("dve", lambda e, tt=tt: e.max(out=top8[:, tt * 8:(tt + 1) * 8], in_=sm[:, tt * 16:(tt + 1) * 16]), [smB], [top8B])
                for tt in range(8):
                    T.op("dve", lambda e, tt=tt: e.tensor_scalar(out=sel[:, tt * 16:(tt + 1) * 16], in0=sm[:, tt * 16:(tt + 1) * 16],
                                                                   scalar1=top8[:, tt * 8 + 2:tt * 8 + 3], scalar2=None, op0=ALU.is_ge), [smB, top8B], [selB])
                T.op("dve", lambda e: e.tensor_tensor(out=sel, in0=sel, in1=past_s[:], op=ALU.mult), [selB, cB], [selB])
                T.op("dve", lambda e: e.tensor_scalar(out=selb2[h % 2], in0=sel, scalar1=-1.0, scalar2=-NEG, op0=ALU.add, op1=ALU.mult), [selB], [selb2B[h % 2]])

            def sel_finish(h):
                selbT, SBTb = selbT2[h % 2], SBT2b[h % 2]
                for tt in range(8):
                    T.op("pe", lambda e, tt=tt: e.transpose(out=PT[0:16, tt * 128:(tt + 1) * 128], in_=selb2[h % 2][:, tt * 16:(tt + 1) * 16], identity=identb[:]),
                         [selb2B[h % 2], cB], [PTb])
                evac(selbT[0:16, :], PT[0:16, :], [PTb], [SBTb], eng="dve")

            def attn_head(h, hook=None):
                q, qB = qT[h % 2], QTb[h % 2]
                kt, ktB = kt_sb[h % 2], KTb[h % 2]
                vh, vhB = vh_sb[h % 2], VHb[h % 2]
                selbT, SBTb = selbT2[h % 2], SBT2b[h % 2]
                units = []
                for n in list(range(4, 16)) + [0, 1, 2, 3]:
                    smin = (n - 4) // 3 if n >= 4 else n
                    c0 = 256 * smin
                    chunks = [(c0, 512), (512, 1024)] if c0 < 512 else [(c0, 1024)]
                    for kt_i in range(2):
                        for (a, b) in chunks:
                            units.append((n, kt_i, a, b))
                last_bank = {}
                for ui, (n, kt_i, a, b) in enumerate(units):
                    last_bank[0 if a < 512 else 1] = ui
                pend = []

                def score(ui, n, kt_i, a, b):
                    w = b - a
                    ps, pb = rotS.next()
                    mm(ps[:, 0:w], kt[:, n * 256 + kt_i * 128: n * 256 + (kt_i + 1) * 128], q[:, a:b], True, False, [ktB, qB], [pb])
                    if n < 4:
                        d0 = 256 * n
                        has_diag = a <= d0 < b
                        rest = [(x0, x1) for (x0, x1) in ((a, min(b, d0)), (max(a, d0 + 256), b)) if x1 > x0] if has_diag else [(a, b)]
                        if has_diag:
                            mm(ps[:, d0 - a:d0 - a + 256], identb[:], cbb[:, kt_i, :], False, len(rest) == 0, [cB], [pb])
                        for ri, (x0, x1) in enumerate(rest):
                            mm(ps[:, x0 - a:x1 - a], eselb[:, n * 128:(n + 1) * 128], selbT[:, x0:x1], False, ri == len(rest) - 1, [cB, SBTb], [pb])
                    else:
                        mm(ps[:, 0:w], eselb[:, n * 128:(n + 1) * 128], selbT[:, a:b], False, True, [cB, SBTb], [pb])
                    k = pi[0] % 4
                    pi[0] += 1
                    T.op("act", lambda e, ps=ps, k=k, w=w: e.activation(out=pT[k][:, 0:w], in_=ps[:, 0:w], func=AF.Exp, scale=SCALE_A), [pb], [PTTb[k]])
                    return (ui, n, kt_i, a, b, k)

                def pv(ui, n, kt_i, a, b, k):
                    bank = 0 if a < 512 else 1
                    base = 512 * bank
                    w = b - a
                    first = ui < 2 and kt_i == 0 and n == 4
                    last = last_bank[bank] == ui
                    mm(accO[bank][:, a - base:b - base], vh[:, n * 2 + kt_i, :], pT[k][:, 0:w], first, last, [vhB, PTTb[k]], [accOb[bank]])
                    mm(accD[bank][:, a - base:b - base], onesb[:], pT[k][:, 0:w], first, last, [cB, PTTb[k]], [accDb[bank]])
                    if last:
                        cs = slice(base, base + 512)
                        T.op("dve", lambda e, bank=bank: e.reciprocal(out=rden[:], in_=accD[bank][:, :]), [accDb[bank]], [rdenB])
                        T.op("dve", lambda e, bank=bank: e.tensor_tensor(out=otmp[:], in0=accO[bank][:, :], in1=rden[:], op=ALU.mult), [accOb[bank], rdenB], [otmpB])
                        T.op("dve", lambda e, cs=cs, h=h: e.tensor_tensor(out=Yv[:, h, cs], in0=otmp[:], in1=sz[h % 2][:, cs], op=ALU.mult),
                             [otmpB, SZb[h % 2]], [Yb[h]])

                for ui, (n, kt_i, a, b) in enumerate(units):
                    pend.append(score(ui, n, kt_i, a, b))
                    if len(pend) > 2:
                        pv(*pend.pop(0))
                    if ui == 14 and hook is not None:
                        hook()
                while pend:
                    pv(*pend.pop(0))

            proj_head_a(0)
            proj_head_a(1)
            sel_finish(0)
            for h in range(8):
                if h + 1 < 8 and h > 0:
                    proj_head_a(h + 1)
                if h == 2:
                    for half in range(2):
                        for kh in range(2):
                            T.dma("pool", mgB[:, half, kh * 8:(kh + 1) * 8, :], wBvg[half][:, kh * 8:(kh + 1) * 8, :], [], [MGb[half]], "mg%d" % half)
                attn_head(h, hook=(lambda h=h: sel_finish(h + 1)) if h + 1 < 8 else None)
            prefetch("u0", wA[32])
            prefetch("zg0", wA[48])
            T.barrier()
            chk("B1")

            wvg_sb = mgB
            vn_sb = L(0, 8192).rearrange("p (t c) -> p t c", t=8)
            VNb = [Buf("vn%d" % i) for i in range(8)]
            gv = L(16384, 1024, F32)
            vn0 = L(20480, 1024, F32)
            lnvg_b = L(24576, 1024, F32)
            lnvb_b = L(28672, 1024, F32)
            bsb = L(32768, 1024, F32)
            gu = [L(36864 + i * 2048, 1024) for i in range(2)]
            szg = [L(40960 + i * 2048, 1024) for i in range(2)]
            wst = L(45056, 1024, F32).rearrange("p (g t) -> p g t", g=8)
            bst = sbt("bst", [128, 4, 6], F32)
            mvs = sbt("mvs", [128, 2], F32)
            rstd = sbt("rstd", [128, 1], F32)
            trilf = sbt("trilf", [128, 128], F32)
            mtmp = sbt("mtmp", [128, 512], F32)
            gvB, vn0B, c2B, bstB, mvsB, rstdB, mtmpB = [Buf(n) for n in "gv vn0 c2 bst mvs rstd mtmp".split()]
            GUb = [Buf("gu%d" % i) for i in range(2)]
            SGb = [Buf("szg%d" % i) for i in range(2)]
            T.dma("sp", lnvg_b, lnvg_d.partition_broadcast(128), [], [c2B], "c2")
            T.dma("sp", lnvb_b, lnvb_d.partition_broadcast(128), [], [c2B], "c2")
            T.dma("sp", bsb, bs_d.partition_broadcast(128), [], [c2B], "c2")
            T.dma("sp", wst, wsT_d, [], [c2B], "c2")
            T.dma("sp", trilf[:], tril_d, [], [c2B], "c2")
            T.op("dve", lambda e: e.tensor_tensor(out=wcb[:], in0=wst, in1=trilf[:].unsqueeze(1).broadcast_to([128, 8, 128]), op=ALU.mult), [c2B], [c2B])
            rotB = Rot([0, 1, 2, 3])
            for tt in range(8):
                for half in range(2):
                    ps, pb = rotB.next()
                    for kc in range(16):
                        mm(ps[:, :], xT_own[:, kc, tt * 128:(tt + 1) * 128], wvg_sb[:, half, kc, :], kc == 0, kc == 15,
                           [MGb[half], XTb[tt // 2]], [pb])
                    evac(gv[:, half * 512:(half + 1) * 512], ps[:, :], [pb], [gvB], func=AF.Gelu_apprx_tanh)
                for i in range(2):
                    T.op("dve", lambda e, i=i: e.bn_stats(out=bst[:, i, :], in_=gv[:, i * 512:(i + 1) * 512]), [gvB], [bstB])
                T.op("dve", lambda e: e.bn_aggr(out=mvs[:], in_=bst[:, 0:2, :]), [bstB], [mvsB])
                T.op("act", lambda e: e.activation(out=rstd[:], in_=mvs[:, 1:2], func=AF.Sqrt, bias=LN_EPS, scale=1.0), [mvsB], [rstdB])
                T.op("dve", lambda e: e.reciprocal(out=rstd[:], in_=rstd[:]), [rstdB], [rstdB])
                T.op("dve", lambda e: e.tensor_scalar(out=vn0, in0=gv, scalar1=mvs[:, 0:1], scalar2=rstd[:, 0:1], op0=ALU.subtract, op1=ALU.mult),
                     [gvB, mvsB, rstdB], [vn0B])
                T.op("dve", lambda e: e.tensor_tensor(out=vn0, in0=vn0, in1=lnvg_b, op=ALU.mult), [vn0B, c2B], [vn0B])
                T.op("dve", lambda e, tt=tt: e.tensor_tensor(out=vn_sb[:, tt, :], in0=vn0, in1=lnvb_b, op=ALU.add), [vn0B, c2B], [VNb[tt]])

            def proj_g(g):
                proj_fm(wA[32 + g], gu[g % 2], GUb[g % 2], func=AF.Gelu_apprx_tanh, rot=rotB, key="u%d" % g)
                proj_fm(wA[48 + g], szg[g % 2], SGb[g % 2], func=AF.Silu, rot=rotB, key="zg%d" % g)

            proj_g(0)
            for g in range(8):
                if g + 1 < 8:
                    proj_g(g + 1)
                if g == 4:
                    for half in range(2):
                        for kh in range(2):
                            T.dma("pool", mgB[:, half, kh * 8:(kh + 1) * 8, :], wmv[half][:, kh * 8:(kh + 1) * 8, :], [], [MGb[half]], "mg%d" % half)
                for hb in range(2):
                    psM, psMb = PS[4 + hb], Pb[4 + hb]
                    for t4 in range(4):
                        tt = hb * 4 + t4
                        mm(psM[:, t4 * 128:(t4 + 1) * 128], vn_sb[:, tt, g * 128:(g + 1) * 128], wcb[:, g, :], True, True, [VNb[tt], c2B], [psMb])
                    T.op("dve", lambda e, psM=psM, g=g: e.tensor_tensor(out=mtmp[:].rearrange("p (a b) -> p a b", a=4), in0=psM[:, :].rearrange("p (a b) -> p a b", a=4),
                                                                      in1=bsb[:, g * 128:(g + 1) * 128].unsqueeze(1).broadcast_to([128, 4, 128]), op=ALU.add),
                         [psMb, c2B], [mtmpB])
                    T.op("dve", lambda e, g=g, hb=hb: e.tensor_tensor(out=mtmp[:], in0=mtmp[:], in1=gu[g % 2][:, hb * 512:(hb + 1) * 512], op=ALU.mult), [mtmpB, GUb[g % 2]], [mtmpB])
                    T.op("dve", lambda e, g=g, hb=hb: e.tensor_tensor(out=Yv[:, 8 + g, hb * 512:(hb + 1) * 512], in0=mtmp[:], in1=szg[g % 2][:, hb * 512:(hb + 1) * 512], op=ALU.mult),
                         [mtmpB, SGb[g % 2]], [Yb[8 + g]])
            prefetch("mk0", wmk[0])
            prefetch("mk1", wmk[1])
            T.barrier()
            chk("B2")

            memT = L(0, 4096).rearrange("p (k m) -> p k m", k=16)
            mkT = L(8192, 2048).rearrange("p (c m) -> p c m", c=8)
            mvv = L(12288, 2048).rearrange("p (t c) -> p t c", t=2)
            qc = [L(16384 + i * 4096, 2048).rearrange("p (c t) -> p c t", c=2) for i in range(2)]
            szc = [L(24576 + i * 4096, 2048).rearrange("p (c t) -> p c t", c=2) for i in range(2)]
            ptc = [L(32768 + i * 1024, 512) for i in range(3)]
            rdc = L(36864, 512, F32)
            otc = L(38912, 512, F32)
            otc2 = [otc, L(40960, 512, F32)]
            otc2B = [Buf("otc0"), Buf("otc1")]
            memB, mkB, mvB, rdcB, otcB = [Buf(n) for n in "memT mkT mvv rdc otc".split()]
            QCb = [Buf("qc%d" % i) for i in range(2)]
            SCb = [Buf("szc%d" % i) for i in range(2)]
            PCb = [Buf("ptc%d" % i) for i in range(3)]
            T.dma("pool", memT, memTl, [], [memB], "memT")
            rotC = Rot([0, 1, 2, 3])
            for cg in range(8):
                w, wB = stream(wmk[cg], "mk%d" % cg)
                ps, pb = rotC.next()
                for kc in range(16):
                    mm(ps[:, 0:256], w[:, kc, :], memT[:, kc, :], kc == 0, kc == 15, [wB, memB], [pb])
                evac(mkT[:, cg, :], ps[:, 0:256], [pb], [mkB])
            for mt in range(2):
                for half in range(2):
                    ps, pb = rotC.next()
                    for kc in range(16):
                        mm(ps[:, :], memT[:, kc, mt * 128:(mt + 1) * 128], mgB[:, half, kc, :], kc == 0, kc == 15, [memB, MGb[half]], [pb])
                    evac(mvv[:, mt, half * 512:(half + 1) * 512], ps[:, :], [pb], [mvB])

            def proj_c(hc):
                for dc in range(2):
                    proj_fm(wA[56 + 2 * hc + dc], qc[hc % 2][:, dc, :], QCb[hc % 2], rot=rotC)
                    proj_fm(wA[64 + 2 * hc + dc], szc[hc % 2][:, dc, :], SCb[hc % 2], func=AF.Silu, rot=rotC)

            proj_c(0)
            pcq = [0]
            pendc = []

            def score_c(hc, half, mt):
                hs = slice(half * 512, (half + 1) * 512)
                ps, pb = rotC.next()
                for dc in range(2):
                    mm(ps[:, :], mkT[:, 2 * hc + dc, mt * 128:(mt + 1) * 128], qc[hc % 2][:, dc, hs], dc == 0, dc == 1, [mkB, QCb[hc % 2]], [pb])
                k = pcq[0] % 3
                pcq[0] += 1
                T.op("act", lambda e, ps=ps, k=k: e.activation(out=ptc[k], in_=ps[:, :], func=AF.Exp, scale=SCALE_C), [pb], [PCb[k]])
                return (hc, half, mt, k)

            def pv_c(hc, half, mt, k):
                hs = slice(half * 512, (half + 1) * 512)
                for dc in range(2):
                    mm(PS[4 + dc][:, :], mvv[:, mt, (2 * hc + dc) * 128:(2 * hc + dc + 1) * 128], ptc[k], mt == 0, mt == 1, [mvB, PCb[k]], [Pb[4 + dc]])
                mm(PS[6][:, :], onesb[:], ptc[k], mt == 0, mt == 1, [cB, PCb[k]], [Pb[6]])
                if mt == 1:
                    T.op("act", lambda e: e.copy(out=rdc, in_=PS[6][:, :]), [Pb[6]], [rdcB])
                    for dc in range(2):
                        T.op("act", lambda e, dc=dc: e.copy(out=otc2[dc], in_=PS[4 + dc][:, :]), [Pb[4 + dc]], [otc2B[dc]])
                    T.op("dve", lambda e: e.reciprocal(out=rdc, in_=rdc), [rdcB], [rdcB])
                    for dc in range(2):
                        T.op("dve", lambda e, dc=dc: e.tensor_tensor(out=otc2[dc], in0=otc2[dc], in1=rdc, op=ALU.mult), [otc2B[dc], rdcB], [otc2B[dc]])
                        T.op("dve", lambda e, dc=dc, hc=hc, hs=hs: e.tensor_tensor(out=Yv[:, 16 + 2 * hc + dc, hs], in0=otc2[dc], in1=szc[hc % 2][:, dc, hs], op=ALU.mult),
                             [otc2B[dc], SCb[hc % 2]], [Yb[16 + 2 * hc + dc]])

            for hc in range(4):
                if hc + 1 < 4:
                    proj_c(hc + 1)
                for half in range(2):
                    for mt in range(2):
                        pendc.append(score_c(hc, half, mt))
                        if len(pendc) > 2:
                            pv_c(*pendc.pop(0))
                while pendc:
                    pv_c(*pendc.pop(0))
            prefetch("g0_0", wA[72])
            prefetch("g1_0", wA[88])
            T.barrier()
            chk("B3")

            WQ = [R_LW[:, 16384:24576].rearrange("p (k n) -> p k n", k=16),
                  R_LW[:, 0:8192].rearrange("p (k n) -> p k n", k=16),
                  R_LW[:, 8192:16384].rearrange("p (k n) -> p k n", k=16)]
            WQb = [Buf("wq%d" % i) for i in range(3)]
            gate = [L(i * 4096, 1024, F32) for i in range(3)]
            GTb = [Buf("gate%d" % i) for i in range(3)]
            wbr_sb = [[L(12288 + (p * 3 + br) * 2048, 1024).rearrange("p (k n) -> p k n", k=8) for br in range(3)] for p in range(2)]
            WBRb = [[Buf("wbr%d_%d" % (p, br)) for br in range(3)] for p in range(2)]
            m0 = L(24576, 512, F32)
            m1 = L(26624, 512, F32)
            m0B, m1B = Buf("m0"), Buf("m1")
            rotG = Rot([0, 1, 2, 3])
            for fc in range(16):
                par = fc % 2
                if fc == 6:
                    for kh in range(2):
                        T.dma("pool", WQ[0][:, kh * 8:(kh + 1) * 8, :], wo[0][:, kh * 8:(kh + 1) * 8, :], [], [WQb[0]], "wq0")
                for br in range(3):
                    T.dma("pool", wbr_sb[par][br], wbr[br, fc], [], [WBRb[par][br]], "wbr%d_%d" % (par, br))
                for br in range(3):
                    proj_fm(wA[72 + br * 16 + fc], gate[br], GTb[br], func=AF.Sigmoid, rot=rotG, key="g%d_%d" % (br, fc))
                for half in range(2):
                    hs = slice(half * 512, (half + 1) * 512)
                    for br in range(3):
                        for kc in range(8):
                            mm(PS[4 + br][:, :], wbr_sb[par][br][:, kc, :], Yv[:, br * 8 + kc, hs], kc == 0, kc == 7, [WBRb[par][br], Yb[br * 8 + kc]], [Pb[4 + br]])
                    T.op("dve", lambda e, hs=hs: e.tensor_tensor(out=m0, in0=PS[4][:, :], in1=gate[0][:, hs], op=ALU.mult), [Pb[4], GTb[0]], [m0B])
                    T.op("dve", lambda e, hs=hs: e.tensor_tensor(out=m1, in0=PS[5][:, :], in1=gate[1][:, hs], op=ALU.mult), [Pb[5], GTb[1]], [m1B])
                    T.op("dve", lambda e: e.tensor_tensor(out=m0, in0=m0, in1=m1, op=ALU.add), [m0B, m1B], [m0B])
                    T.op("dve", lambda e, hs=hs: e.tensor_tensor(out=m1, in0=PS[6][:, :], in1=gate[2][:, hs], op=ALU.mult), [Pb[6], GTb[2]], [m1B])
                    T.op("dve", lambda e, fc=fc, hs=hs: e.tensor_tensor(out=merged[:, fc, hs], in0=m0, in1=m1, op=ALU.add), [m0B, m1B], [MRb[fc]])
            T.barrier()
            chk("C")

            for q4 in (1, 2):
                for kh in range(2):
                    T.dma("pool", WQ[q4][:, kh * 8:(kh + 1) * 8, :], wo[q4][:, kh * 8:(kh + 1) * 8, :], [], [WQb[q4]], "wq%d" % q4)
            lng_b = R_LW[:, 24576:28672].bitcast(F32)
            lnb_b = R_LW[:, 28672:32768].bitcast(F32)
            c3B = Buf("c3")
            T.dma("sp", lng_b, lng_d.partition_broadcast(128), [], [c3B], "c3")
            T.dma("sp", lnb_b, lnb_d.partition_broadcast(128), [], [c3B], "c3")
            rr = [R_XT[:, i * 4096:(i + 1) * 4096].bitcast(F32) for i in range(4)] + \
                 [R_Y[:, i * 4096:(i + 1) * 4096].bitcast(F32) for i in range(4)]
            RRb = [Buf("rr%d" % i) for i in range(8)]
            xt = [R_Y[:, 16384 + i * 4096: 16384 + (i + 1) * 4096].bitcast(F32) for i in range(2)]
            XIb = [Buf("xi%d" % i) for i in range(2)]
            nbias = sbt("nbias", [128, 1], F32)
            nbB = Buf("nbias")
            outB = Buf("out")
            rotD = Rot([0, 1, 2, 3, 4, 5, 6])

            def out_q(tt, q4):
                ps, pb = rotD.next()
                cs = slice(q4 * 512, (q4 + 1) * 512)
                wq, wqB = (WQ[q4], WQb[q4]) if q4 < 3 else (WQ[0], WQb[0])
                for kc in range(16):
                    mm(ps[:, :], merged[:, kc, tt * 128:(tt + 1) * 128], wq[:, kc, :], kc == 0, kc == 15, [MRb[kc], wqB], [pb])
                T.op("act", lambda e, ps=ps, tt=tt, cs=cs: e.copy(out=rr[tt][:, cs], in_=ps[:, :]), [pb], [RRb[tt]])

            for q4 in (0, 1):
                for tt in range(8):
                    out_q(tt, q4)
            for kh in range(2):
                T.dma("pool", WQ[0][:, kh * 8:(kh + 1) * 8, :], wo[3][:, kh * 8:(kh + 1) * 8, :], [], [WQb[0]], "wq0")
            bst2 = [sbt("bst2_%d" % i, [128, 4, 6], F32) for i in range(2)]
            mvs2 = [sbt("mvs2_%d" % i, [128, 2], F32) for i in range(2)]
            rstd2 = [sbt("rstd2_%d" % i, [128, 1], F32) for i in range(2)]
            nb2 = [sbt("nb2_%d" % i, [128, 1], F32) for i in range(2)]
            bst2B = [Buf("bst2_%d" % i) for i in range(2)]
            mvs2B = [Buf("mvs2_%d" % i) for i in range(2)]
            rstd2B = [Buf("rstd2_%d" % i) for i in range(2)]
            nb2B = [Buf("nb2_%d" % i) for i in range(2)]

            def stage1(tt):
                sl = tt % 2
                T.dma("sp", xt[sl], xown[tt], [], [XIb[sl]], "xi%d" % sl)
                for q4 in (2, 3):
                    out_q(tt, q4)

            def stage2(tt):
                sl = tt % 2
                T.op("dve", lambda e: e.scalar_tensor_tensor(out=rr[tt], in0=xt[sl], scalar=DN_ALPHA, in1=rr[tt], op0=ALU.mult, op1=ALU.add),
                     [RRb[tt], XIb[sl]], [RRb[tt]])
                for q4 in range(4):
                    T.op("dve", lambda e, q4=q4: e.bn_stats(out=bst2[sl][:, q4, :], in_=rr[tt][:, q4 * 512:(q4 + 1) * 512]), [RRb[tt]], [bst2B[sl]])
                T.op("dve", lambda e: e.bn_aggr(out=mvs2[sl][:], in_=bst2[sl][:, :, :]), [bst2B[sl]], [mvs2B[sl]])
                T.op("act", lambda e: e.activation(out=rstd2[sl][:], in_=mvs2[sl][:, 1:2], func=AF.Sqrt, bias=LN_EPS, scale=1.0), [mvs2B[sl]], [rstd2B[sl]])
                T.op("dve", lambda e: e.reciprocal(out=rstd2[sl][:], in_=rstd2[sl][:]), [rstd2B[sl]], [rstd2B[sl]])
                T.op("dve", lambda e: e.scalar_tensor_tensor(out=nb2[sl][:], in0=mvs2[sl][:, 0:1], scalar=-1.0, in1=rstd2[sl][:], op0=ALU.mult, op1=ALU.mult),
                     [mvs2B[sl], rstd2B[sl]], [nb2B[sl]])

            def stage3a(tt):
                sl = tt % 2
                T.op("act", lambda e: e.activation(out=rr[tt], in_=rr[tt], func=AF.Identity, bias=nb2[sl][:, 0:1], scale=rstd2[sl][:, 0:1]),
                     [RRb[tt], nb2B[sl], rstd2B[sl]], [RRb[tt]])
                T.op("pool", lambda e: e.tensor_tensor(out=rr[tt], in0=rr[tt], in1=lng_b, op=ALU.mult), [RRb[tt], c3B], [RRb[tt]])

            def stage3b(tt):
                T.op("dve" if tt % 2 == 0 else "pool", lambda e: e.tensor_tensor(out=rr[tt], in0=rr[tt], in1=lnb_b, op=ALU.add), [RRb[tt], c3B], [RRb[tt]])
                T.dma("sp", out_d[tt], rr[tt], [RRb[tt]], [outB], "o%d" % (tt % 2))

            stage1(0)
            for tt in range(8):
                if tt + 1 < 8:
                    stage1(tt + 1)
                stage2(tt)
                stage3a(tt)
                if tt >= 1:
                    stage3b(tt - 1)
            stage3b(7)
            T.op("sp", lambda e: None, [outB], [])
        try:
            _phases()
        except _Stop:
            import os
            if os.environ.get('NOBAR'):
                for e_ in ENGS:
                    T.op(e_, lambda e: None, [cB], [])
            else:
                T.barrier()
        T.emit(st)
    return nc


def _core_layout(c):
    b, j = c // 4, c % 4
    own = [j, 7 - j, 8 + j, 15 - j]
    others = [n for n in range(16) if n not in own]
    order = own + others
    return b, own, order


_CONST = {}


def _consts():
    if _CONST:
        return _CONST
    ident = np.eye(128, dtype=np.float32)
    esel = np.zeros((128, 16, 128), np.float32)
    for n in range(16):
        esel[n, n, :] = 1.0
    p = np.arange(128)[:, None, None]
    kt = np.arange(2)[None, :, None]
    f = np.arange(256)[None, None, :]
    cb = np.where(kt * 128 + p <= f, 0.0, NEG).astype(np.float32)
    s = np.arange(128)[:, None]
    t = np.arange(128)[None, :]
    tril = (s <= t).astype(np.float32)
    _CONST.update(ident=ident, esel=esel.reshape(128, 2048), cb=np.ascontiguousarray(cb), tril=tril)
    return _CONST


def _lay_A(w):
    K, C = w.shape
    return np.ascontiguousarray(w.reshape(K // 128, 128, C // 128, 128).transpose(2, 1, 0, 3))


def _lay_B(w):
    K, C = w.shape
    return np.ascontiguousarray(w.reshape(K // 128, 128, C // 512, 512).transpose(2, 1, 0, 3))


def make_in_maps(x, mem, w_in, w_mem_k, w_mem_v, w_s, b_s, ln_v_g, ln_v_b,
                 w_branch_attn, w_branch_sgu, w_branch_mem, w_out, ln_g, ln_b, cores=range(8)):
    f = lambda a: np.asarray(a, dtype=np.float32)
    x, mem, w_in = f(x), f(mem), f(w_in)
    shared = dict(_consts())
    shared["wA"] = _lay_A(w_in)
    shared["wBva"] = _lay_B(w_in[:, 2048:3072])
    shared["wBvg"] = _lay_B(w_in[:, 5120:6144])
    shared["wmk"] = _lay_A(f(w_mem_k))
    shared["wmv"] = _lay_B(f(w_mem_v))
    shared["wbr"] = np.stack([_lay_A(f(w)) for w in (w_branch_attn, w_branch_sgu, w_branch_mem)])
    shared["wo"] = _lay_B(f(w_out))
    shared["wsT"] = np.ascontiguousarray(f(w_s).transpose(2, 0, 1))
    shared["bs"] = np.ascontiguousarray(f(b_s).reshape(1, 1024))
    shared["lnvg"] = f(ln_v_g).reshape(1, 1024)
    shared["lnvb"] = f(ln_v_b).reshape(1, 1024)
    shared["lng"] = f(ln_g).reshape(1, 2048)
    shared["lnb"] = f(ln_b).reshape(1, 2048)
    maps = []
    for c in cores:
        b, own, order = _core_layout(c)
        perm = np.concatenate([np.arange(n * 256, (n + 1) * 256) for n in order])
        xp = x[b][perm]
        m = dict(shared)
        m["xTl"] = np.ascontiguousarray(xp.reshape(16, 256, 16, 128).transpose(0, 3, 2, 1))
        m["xown"] = np.ascontiguousarray(xp[:1024].reshape(8, 128, 2048))
        qblk = np.repeat(np.array(own), 256)
        past = (np.array(order)[None, :] < qblk[:, None]).astype(np.float32)
        m["past"] = np.ascontiguousarray(past.reshape(8, 128, 16).transpose(1, 0, 2).reshape(128, 128))
        m["memTl"] = np.ascontiguousarray(mem[b].T.reshape(16, 128, 256).transpose(1, 0, 2))
        maps.append(m)
    return maps


_NC = []


def kernel(**inputs):
    if not _NC:
        _NC.append(build_program())
    nc = _NC[0]
    in_maps = make_in_maps(**inputs)
    res = run_bass_kernel_spmd(nc, in_maps, core_ids=list(range(8)))
    out = np.zeros((2, 4096, 2048), np.float32)
    for c in range(8):
        b, own, order = _core_layout(c)
        o = np.asarray(res.results[c]["out"]).reshape(1024, 2048)
        for s, n in enumerate(own):
            out[b, n * 256:(n + 1) * 256] = o[s * 256:(s + 1) * 256]
    return out
```

```python
import concourse.bass as bass
import concourse.mybir as mybir

ENGS = ("pe", "act", "dve", "pool", "sp")


class Buf:
    __slots__ = ("name", "w", "r_eng", "r_dma")

    def __init__(self, name):
        self.name = name
        self.w = None
        self.r_eng = {}
        self.r_dma = []


class Op:
    __slots__ = ("eng", "fn", "is_dma", "key", "val", "mark", "idx", "deps", "waits", "raw")

    def __init__(self, eng, fn, is_dma, key):
        self.eng = eng
        self.fn = fn
        self.is_dma = is_dma
        self.key = key
        self.val = 0
        self.mark = False
        self.idx = -1
        self.deps = []
        self.waits = []
        self.raw = set()


class Tracker:
    def __init__(self, nc):
        self.nc = nc
        self.ops = {e: [] for e in ENGS}
        self.dma_cnt = {}
        self.nops = 0
        self.last_dma = {}

    def op(self, eng, fn, reads=(), writes=(), dma_key=None):
        o = Op(eng, fn, dma_key is not None, dma_key)
        deps = []
        for b in reads:
            if b.w is not None:
                deps.append(b.w)
                o.raw.add(id(b.w))
        for b in writes:
            if b.w is not None:
                deps.append(b.w)
            deps.extend(b.r_eng.values())
            deps.extend(b.r_dma)
        seen = set()
        for d in deps:
            if d is o or id(d) in seen:
                continue
            seen.add(id(d))
            if (not d.is_dma) and (not o.is_dma) and d.eng == eng:
                if eng == "pe":
                    continue
            o.deps.append(d)
        for b in reads:
            if o.is_dma:
                b.r_dma.append(o)
            else:
                b.r_eng[eng] = o
        for b in writes:
            b.w = o
            b.r_eng = {}
            b.r_dma = []
        if o.is_dma:
            c = self.dma_cnt.get(dma_key, 0) + 16
            self.dma_cnt[dma_key] = c
            o.val = c
            self.last_dma[dma_key] = o
        o.idx = len(self.ops[eng])
        self.ops[eng].append(o)
        self.nops += 1
        return o

    def dma(self, eng, out, in_, reads, writes, key, **kw):
        return self.op(eng, lambda e: e.dma_start(out=out, in_=in_, **kw), reads, writes, dma_key=key)

    def barrier(self, skip=("ws", "kt0", "vh0", "mg", "wq")):
        lasts = []
        for e in ENGS:
            for o in reversed(self.ops[e]):
                if not o.is_dma and o.fn is not None:
                    lasts.append(o)
                    break
        for k, o in self.last_dma.items():
            if any(str(k).startswith(p) for p in skip):
                continue
            lasts.append(o)
        for e in ENGS:
            b = Op(e, None, False, None)
            b.deps = list(lasts)
            b.idx = len(self.ops[e])
            self.ops[e].append(b)

    def finalize_marks(self):
        for e in ENGS:
            seen_idx = {}
            seen_dma = {}
            for o in self.ops[e]:
                for d in o.deps:
                    if d.is_dma:
                        if seen_dma.get(d.key, 0) < d.val:
                            seen_dma[d.key] = d.val
                            o.waits.append(d)
                    else:
                        if seen_idx.get(d.eng, -1) < d.idx:
                            seen_idx[d.eng] = d.idx
                            d.mark = True
                            o.waits.append(d)
        for e in ENGS:
            c = 0
            for o in self.ops[e]:
                if not o.is_dma and o.mark:
                    c += 1
                    o.val = c

    def emit(self, stack):
        nc = self.nc
        self.finalize_marks()
        esem = {e: stack.enter_context(nc.semaphore("s_" + e)) for e in ENGS}
        dsem = {k: stack.enter_context(nc.semaphore("d_%s" % (k,))) for k in self.dma_cnt}

        def run(e, h):
            for o in self.ops[e]:
                for d in o.waits:
                    if d.is_dma:
                        h.wait_ge(dsem[d.key], d.val)
                    else:
                        h.wait_ge(esem[d.eng], d.val)
                ins = o.fn(h) if o.fn is not None else None
                if ins is None:
                    continue
                if o.is_dma:
                    ins.then_inc(dsem[o.key], 16)
                elif o.mark:
                    ins.then_inc(esem[e], 1)

        with nc.Block() as block:
            @block.sync
            def _(h):
                run("sp", h)

            @block.scalar
            def _(h):
                run("act", h)

            @block.vector
            def _(h):
                run("dve", h)

            @block.gpsimd
            def _(h):
                run("pool", h)

            @block.tensor
            def _(h):
                run("pe", h)

import numpy as np
from contextlib import ExitStack
from concourse.bass_utils import run_bass_kernel_spmd

F32 = mybir.dt.float32
BF16 = mybir.dt.bfloat16
AF = mybir.ActivationFunctionType
ALU = mybir.AluOpType
AX = mybir.AxisListType

NEG = -30000.0
BIGM = 1.0e4
SCALE_A = 128.0 ** -0.5
SCALE_C = 256.0 ** -0.5
DN_ALPHA = 2.0 ** 0.25
LN_EPS = 1e-5


def slot_positions(s):
    return list(range(s)) + list(range(4, 4 + 3 * (s + 1)))


class _Stop(Exception):
    pass


def build_program(stop=None):
    nc = bass.Bass("TRN2", target_bir_lowering=False)
    di = lambda name, shape: nc.dram_tensor(name, shape, F32, kind="ExternalInput").ap()
    xTl = di("xTl", [16, 128, 16, 256])
    xown = di("xown", [8, 128, 2048])
    past_d = di("past", [128, 128])
    memTl = di("memTl", [128, 16, 256])
    wA = di("wA", [120, 128, 16, 128])
    wBva = di("wBva", [2, 128, 16, 512])
    wBvg = di("wBvg", [2, 128, 16, 512])
    wmk = di("wmk", [8, 128, 16, 128])
    wmv = di("wmv", [2, 128, 16, 512])
    wbr = di("wbr", [3, 16, 128, 8, 128])
    wo = di("wo", [4, 128, 16, 512])
    wsT_d = di("wsT", [128, 8, 128])
    tril_d = di("tril", [128, 128])
    bs_d = di("bs", [1, 1024])
    lnvg_d = di("lnvg", [1, 1024])
    lnvb_d = di("lnvb", [1, 1024])
    lng_d = di("lng", [1, 2048])
    lnb_d = di("lnb", [1, 2048])
    ident_d = di("ident", [128, 128])
    esel_d = di("esel", [128, 2048])
    cb_d = di("cb", [128, 2, 256])
    out_d = nc.dram_tensor("out", [8, 128, 2048], F32, kind="ExternalOutput").ap()
    kT_d = nc.dram_tensor("kT_scr", [8, 128, 4096], BF16).ap()
    v_d = nc.dram_tensor("v_scr", [8, 32, 128, 128], BF16).ap()

    T = Tracker(nc)
    with ExitStack() as st:
        sbt = lambda name, shape, dt: st.enter_context(nc.sbuf_tensor(name, shape, dt))
        R_XT = sbt("R_XT", [128, 16384], BF16)
        R_Y = sbt("R_Y", [128, 24576], BF16)
        R_MG = sbt("R_MG", [128, 16384], BF16)
        R_LW = sbt("R_LW", [128, 32768], BF16)
        identb = sbt("identb", [128, 128], BF16)
        onesb = sbt("onesb", [128, 128], BF16)
        eselb = sbt("eselb", [128, 2048], BF16)
        cbb = sbt("cbb", [128, 2, 256], BF16)
        past_s = sbt("past_s", [128, 128], F32)
        pm1b = sbt("pm1b", [128, 128], F32)
        kms = sbt("kms", [128, 8, 16], F32)
        kmb = sbt("kmb", [128, 8, 16], BF16)
        wcb = sbt("wcb", [128, 8, 128], BF16)
        PS = [st.enter_context(nc.psum_tensor("ps%d" % i, [128, 512], F32)) for i in range(7)]
        PT = st.enter_context(nc.psum_tensor("pst", [128, 1024], BF16))
        Pb = [Buf("ps%d" % i) for i in range(7)]
        PTb = Buf("pst")

        class Rot:
            def __init__(self, idxs):
                self.idxs = idxs
                self.i = 0

            def next(self):
                k = self.idxs[self.i % len(self.idxs)]
                self.i += 1
                return PS[k], Pb[k]

        evac_i = [0]

        def evac(out, in_, reads, writes, func=None, scale=1.0, eng=None):
            if func is not None:
                T.op("act", lambda e: e.activation(out=out, in_=in_, func=func, scale=scale), reads, writes)
                return
            if eng is None:
                eng = "act" if evac_i[0] % 2 == 0 else "dve"
                evac_i[0] += 1
            if eng == "act":
                T.op("act", lambda e: e.copy(out=out, in_=in_), reads, writes)
            else:
                T.op("dve", lambda e: e.tensor_copy(out=out, in_=in_), reads, writes)

        def mm(out, lhsT, rhs, start, stop, reads, writes):
            T.op("pe", lambda e: e.matmul(out, lhsT=lhsT, rhs=rhs, start=start, stop=stop), reads, writes)

        xT_own = R_XT[:, :].rearrange("p (k t) -> p k t", k=16)
        XTb = [Buf("xt%d" % i) for i in range(4)]
        wk_sb = R_Y[:, 0:16384].rearrange("p (c k n) -> p c k n", c=8, k=16)
        WKb = [Buf("wk%d" % i) for i in range(8)]
        Yv = R_Y[:, :].rearrange("p (c t) -> p c t", c=24)
        Yb = [Buf("y%d" % i) for i in range(24)]
        mgB = R_MG[:, :].rearrange("p (h k n) -> p h k n", h=2, k=16)
        MGb = [Buf("mg%d" % i) for i in range(2)]
        merged = R_MG[:, :].rearrange("p (c t) -> p c t", c=16)
        MRb = [Buf("mr%d" % i) for i in range(16)]
        ws = [R_LW[:, 24576 + i * 2048: 24576 + (i + 1) * 2048].rearrange("p (k n) -> p k n", k=16) for i in range(4)]
        WSb = [Buf("ws%d" % i) for i in range(4)]
        ws_i = [0]

        prefetched = {}

        def prefetch(key, src):
            prefetched[key] = stream(src)

        def stream(src, key=None):
            if key is not None and key in prefetched:
                return prefetched.pop(key)
            slot = ws_i[0] % 4
            ws_i[0] += 1
            T.dma("pool", ws[slot], src, [], [WSb[slot]], "ws%d" % slot)
            return ws[slot], WSb[slot]

        def L(off, n, dt=BF16):
            if dt == BF16:
                return R_LW[:, off // 2: off // 2 + n]
            return R_LW[:, off // 2: off // 2 + 2 * n].bitcast(F32)

        cB = Buf("consts")

        def chk(name):
            if stop == name:
                raise _Stop()
        def _phases():
            import os
            _d0 = os.environ.get("DBG0", "iecpmt")
            if "i" in _d0: T.dma("pool", identb[:], ident_d, [], [cB], "c0")
            if "e" in _d0: T.dma("pool", eselb[:], esel_d, [], [cB], "c0")
            if "c" in _d0: T.dma("pool", cbb[:], cb_d, [], [cB], "c0")
            if "p" in _d0: T.dma("sp", past_s[:], past_d, [], [cB], "c1")
            if "t" in _d0: T.op("dve", lambda e: e.tensor_scalar(out=pm1b[:], in0=past_s[:], scalar1=-1.0, scalar2=BIGM, op0=ALU.add, op1=ALU.mult), [cB], [cB])
            if "m" in _d0: T.op("dve", lambda e: e.memset(onesb[:], 1.0), [], [cB])
            chk("0")
            rotA = Rot([0, 1, 2, 3, 4, 5, 6])
            _da = os.environ.get("DBGA", "kvx")
            T.dma("pool", xT_own[:, :, 0:256], xTl[0], [], [XTb[0]], "xt0")
            for cg in range(8):
                if "k" in _da: T.dma("pool", wk_sb[:, cg], wA[8 + cg], [], [WKb[cg]], "wk%d" % cg)
            for half in range(2):
                for kh in range(2):
                    if "v" in _da: T.dma("pool", mgB[:, half, kh * 8:(kh + 1) * 8, :], wBva[half][:, kh * 8:(kh + 1) * 8, :], [], [MGb[half]], "mg%d" % half)
            chk("AW")
            if os.environ.get("DBGB"):
                T.barrier()
            xr = [L(i * 8192, 4096).rearrange("p (k t) -> p k t", k=16) for i in range(2)]
            XRb = [Buf("xr%d" % i) for i in range(2)]
            kst = [L(16384 + i * 4096, 2048).rearrange("p (h t) -> p h t", h=8) for i in range(2)]
            KSb = [Buf("kst%d" % i) for i in range(2)]
            vst = [L(24576 + i * 4096, 2048).rearrange("p (t c) -> p t c", t=2) for i in range(2)]
            VSb = [Buf("vst%d" % i) for i in range(2)]
            kmsB = Buf("kms")
            kT_v = kT_d.rearrange("h d t -> d h t")
            v_v = v_d.rearrange("h n p d -> p n h d")
            kTdB = Buf("kTd")
            vdB = Buf("vd")
            for ch in range(16):
                chk("A%d" % ch)
                if ch < 4:
                    xc, xcB = xT_own[:, :, ch * 256:(ch + 1) * 256], XTb[ch]
                    if ch > 0:
                        T.dma("pool", xc, xTl[ch], [], [xcB], "xt%d" % ch)
                else:
                    xc, xcB = xr[ch % 2], XRb[ch % 2]
                    T.dma("pool", xc, xTl[ch], [], [xcB], "xr%d" % (ch % 2))
                sl = ch % 2
                for h in range(8):
                    ps, pb = rotA.next()
                    for kc in range(16):
                        mm(ps[:, 0:256], wk_sb[:, h, kc, :], xc[:, kc, :], kc == 0, kc == 15, [WKb[h], xcB], [pb])
                    _dc = os.environ.get("DBGC", "er")
                    if "e" in _dc: evac(kst[sl][:, h, :], ps[:, 0:256], [pb], [KSb[sl]])
                    if "E" in _dc: evac(kst[sl][:, h, :], ps[:, 0:256], [pb], [KSb[sl]], eng="act")
                    if "r" in _dc: T.op("dve", lambda e, h=h, ch=ch, ps=ps: e.reduce_sum(out=kms[:, h, ch:ch + 1], in_=ps[:, 0:256], axis=AX.X), [pb], [kmsB, pb])
                    chk("AK%d" % h)
                chk("AK")
                T.dma("sp", kT_v[:, :, ch * 256:(ch + 1) * 256], kst[sl], [KSb[sl]], [kTdB], "kst%d" % sl)
                chk("AKD")
                for tt in range(2):
                    for half in range(2):
                        ps, pb = rotA.next()
                        for kc in range(16):
                            mm(ps[:, :], xc[:, kc, tt * 128:(tt + 1) * 128], mgB[:, half, kc, :], kc == 0, kc == 15, [MGb[half], xcB], [pb])
                        evac(vst[sl][:, tt, half * 512:(half + 1) * 512], ps[:, :], [pb], [VSb[sl]])
                chk("AV")
                for tt in range(2):
                    T.dma("sp", v_v[:, ch * 2 + tt], vst[sl][:, tt, :].rearrange("p (h d) -> p h d", h=8), [VSb[sl]], [vdB], "vst%d" % sl)
            T.op("dve", lambda e: e.tensor_scalar(out=kmb[:], in0=kms[:], scalar1=1.0 / 256.0, scalar2=None, op0=ALU.mult), [kmsB], [kmsB])
            prefetch("q0", wA[0])
            prefetch("z0", wA[24])
            prefetch("q1", wA[1])
            prefetch("z1", wA[25])
            kt0 = R_LW[:, 16384:20480]
            vh0 = R_LW[:, 20480:24576].rearrange("p (n d) -> p n d", n=32)
            KT0b, VH0b = Buf("kt0"), Buf("vh0")
            T.dma("sp", kt0, kT_d[0], [kTdB], [KT0b], "kt0")
            T.dma("sp", vh0, v_d[0].rearrange("n p d -> p n d"), [vdB], [VH0b], "vh0")
            T.barrier()
            chk("A")

            kt_sb = [L(32768, 4096), L(0, 4096)]
            KTb = [KT0b, Buf("kt1")]
            vh_sb = [L(40960, 4096).rearrange("p (n d) -> p n d", n=32), L(8192, 4096).rearrange("p (n d) -> p n d", n=32)]
            VHb = [VH0b, Buf("vh1")]
            qT = [L(16384 + i * 2048, 1024) for i in range(2)]
            QTb = [Buf("qT%d" % i) for i in range(2)]
            sz = [L(20480 + i * 2048, 1024) for i in range(2)]
            SZb = [Buf("sz%d" % i) for i in range(2)]
            pT = [L(24576 + i * 1024, 512) for i in range(4)]
            PTTb = [Buf("pT%d" % i) for i in range(4)]
            selbT2 = [L(28672 + i * 2048, 1024) for i in range(2)]
            SBT2b = [Buf("selbT%d" % i) for i in range(2)]
            for i in range(2):
                T.op("dve", lambda e, i=i: e.memset(selbT2[i], 0.0), [], [SBT2b[i]])
            sm = sbt("sm", [128, 128], F32)[:]
            sel = sbt("sel", [128, 128], F32)[:]
            top8 = sbt("top8", [128, 64], F32)[:]
            selb2 = [sbt("selb%d" % i, [128, 128], BF16)[:] for i in range(2)]
            selb2B = [Buf("selb%d" % i) for i in range(2)]
            rden = sbt("rden", [128, 512], F32)
            otmp = sbt("otmp", [128, 512], F32)
            pi = [0]
            smB, selB, top8B, selbB, rdenB, otmpB = [Buf(n) for n in "sm sel top8 selb rden otmp".split()]
            rotS = Rot([4, 5, 6])
            rotP = rotS
            accO, accOb = [PS[0], PS[1]], [Pb[0], Pb[1]]
            accD, accDb = [PS[2], PS[3]], [Pb[2], Pb[3]]

            def proj_fm(cg_src, dst, dstB, func=None, rot=None, key=None):
                w, wB = stream(cg_src, key)
                for half in range(2):
                    ps, pb = rot.next()
                    for kc in range(16):
                        mm(ps[:, :], w[:, kc, :], xT_own[:, kc, half * 512:(half + 1) * 512], kc == 0, kc == 15,
                           [wB, XTb[2 * half], XTb[2 * half + 1]], [pb])
                    evac(dst[:, half * 512:(half + 1) * 512], ps[:, :], [pb], [dstB], func=func, eng=None if func is not None else "dve")

            def proj_head_a(h):
                if h > 0:
                    T.dma("sp", kt_sb[h % 2], kT_d[h], [kTdB], [KTb[h % 2]], "kt%d" % (h % 2))
                    T.dma("sp", vh_sb[h % 2], v_d[h].rearrange("n p d -> p n d"), [vdB], [VHb[h % 2]], "vh%d" % (h % 2))
                proj_fm(wA[h], qT[h % 2], QTb[h % 2], rot=rotP, key="q%d" % h)
                proj_fm(wA[24 + h], sz[h % 2], SZb[h % 2], func=AF.Silu, rot=rotP, key="z%d" % h)
                q, qB = qT[h % 2], QTb[h % 2]
                selbT, SBTb = selbT2[h % 2], SBT2b[h % 2]
                psS, psSb = rotS.next()
                for tt in range(8):
                    mm(psS[:, tt * 16:(tt + 1) * 16], q[:, tt * 128:(tt + 1) * 128], kmb[:, h, :], True, True, [qB, kmsB], [psSb])
                T.op("dve", lambda e: e.tensor_tensor(out=sm, in0=psS[:, 0:128], in1=past_s[:], op=ALU.mult), [psSb, cB], [smB])
                T.op("dve", lambda e: e.tensor_tensor(out=sm, in0=sm, in1=pm1b[:], op=ALU.add), [smB, cB], [smB])
                for tt in range(8):
                    T.op("dve", lambda e, tt=tt: e.max(out=top8[:, tt * 8:(tt + 1) * 8], in_=sm[:, tt * 16:(tt + 1) * 16]), [smB], [top8B])
                for tt in range(8):
                    T.op("dve", lambda e, tt=tt: e.tensor_scalar(out=sel[:, tt * 16:(tt + 1) * 16], in0=sm[:, tt * 16:(tt + 1) * 16],
                                                                   scalar1=top8[:, tt * 8 + 2:tt * 8 + 3], scalar2=None, op0=ALU.is_ge), [smB, top8B], [selB])
                T.op("dve", lambda e: e.tensor_tensor(out=sel, in0=sel, in1=past_s[:], op=ALU.mult), [selB, cB], [selB])
                T.op("dve", lambda e: e.tensor_scalar(out=selb2[h % 2], in0=sel, scalar1=-1.0, scalar2=-NEG, op0=ALU.add, op1=ALU.mult), [selB], [selb2B[h % 2]])

            def sel_finish(h):
                selbT, SBTb = selbT2[h % 2], SBT2b[h % 2]
                for tt in range(8):
                    T.op("pe", lambda e, tt=tt: e.transpose(out=PT[0:16, tt * 128:(tt + 1) * 128], in_=selb2[h % 2][:, tt * 16:(tt + 1) * 16], identity=identb[:]),
                         [selb2B[h % 2], cB], [PTb])
                evac(selbT[0:16, :], PT[0:16, :], [PTb], [SBTb], eng="dve")

            def attn_head(h, hook=None):
                q, qB = qT[h % 2], QTb[h % 2]
                kt, ktB = kt_sb[h % 2], KTb[h % 2]
                vh, vhB = vh_sb[h % 2], VHb[h % 2]
                selbT, SBTb = selbT2[h % 2], SBT2b[h % 2]
                units = []
                for n in list(range(4, 16)) + [0, 1, 2, 3]:
                    smin = (n - 4) // 3 if n >= 4 else n
                    c0 = 256 * smin
                    chunks = [(c0, 512), (512, 1024)] if c0 < 512 else [(c0, 1024)]
                    for kt_i in range(2):
                        for (a, b) in chunks:
                            units.append((n, kt_i, a, b))
                last_bank = {}
                for ui, (n, kt_i, a, b) in enumerate(units):
                    last_bank[0 if a < 512 else 1] = ui
                pend = []

                def score(ui, n, kt_i, a, b):
                    w = b - a
                    ps, pb = rotS.next()
                    mm(ps[:, 0:w], kt[:, n * 256 + kt_i * 128: n * 256 + (kt_i + 1) * 128], q[:, a:b], True, False, [ktB, qB], [pb])
                    if n < 4:
                        d0 = 256 * n
                        has_diag = a <= d0 < b
                        rest = [(x0, x1) for (x0, x1) in ((a, min(b, d0)), (max(a, d0 + 256), b)) if x1 > x0] if has_diag else [(a, b)]
                        if has_diag:
                            mm(ps[:, d0 - a:d0 - a + 256], identb[:], cbb[:, kt_i, :], False, len(rest) == 0, [cB], [pb])
                        for ri, (x0, x1) in enumerate(rest):
                            mm(ps[:, x0 - a:x1 - a], eselb[:, n * 128:(n + 1) * 128], selbT[:, x0:x1], False, ri == len(rest) - 1, [cB, SBTb], [pb])
                    else:
                        mm(ps[:, 0:w], eselb[:, n * 128:(n + 1) * 128], selbT[:, a:b], False, True, [cB, SBTb], [pb])
                    k = pi[0] % 4
                    pi[0] += 1
                    T.op("act", lambda e, ps=ps, k=k, w=w: e.activation(out=pT[k][:, 0:w], in_=ps[:, 0:w], func=AF.Exp, scale=SCALE_A), [pb], [PTTb[k]])
                    return (ui, n, kt_i, a, b, k)

                def pv(ui, n, kt_i, a, b, k):
                    bank = 0 if a < 512 else 1
                    base = 512 * bank
                    w = b - a
                    first = ui < 2 and kt_i == 0 and n == 4
                    last = last_bank[bank] == ui
                    mm(accO[bank][:, a - base:b - base], vh[:, n * 2 + kt_i, :], pT[k][:, 0:w], first, last, [vhB, PTTb[k]], [accOb[bank]])
                    mm(accD[bank][:, a - base:b - base], onesb[:], pT[k][:, 0:w], first, last, [cB, PTTb[k]], [accDb[bank]])
                    if last:
                        cs = slice(base, base + 512)
                        T.op("dve", lambda e, bank=bank: e.reciprocal(out=rden[:], in_=accD[bank][:, :]), [accDb[bank]], [rdenB])
                        T.op("dve", lambda e, bank=bank: e.tensor_tensor(out=otmp[:], in0=accO[bank][:, :], in1=rden[:], op=ALU.mult), [accOb[bank], rdenB], [otmpB])
                        T.op("dve", lambda e, cs=cs, h=h: e.tensor_tensor(out=Yv[:, h, cs], in0=otmp[:], in1=sz[h % 2][:, cs], op=ALU.mult),
                             [otmpB, SZb[h % 2]], [Yb[h]])

                for ui, (n, kt_i, a, b) in enumerate(units):
                    pend.append(score(ui, n, kt_i, a, b))
                    if len(pend) > 2:
                        pv(*pend.pop(0))
                    if ui == 14 and hook is not None:
                        hook()
                while pend:
                    pv(*pend.pop(0))

            proj_head_a(0)
            proj_head_a(1)
            sel_finish(0)
            for h in range(8):
                if h + 1 < 8 and h > 0:
                    proj_head_a(h + 1)
                if h == 2:
                    for half in range(2):
                        for kh in range(2):
                            T.dma("pool", mgB[:, half, kh * 8:(kh + 1) * 8, :], wBvg[half][:, kh * 8:(kh + 1) * 8, :], [], [MGb[half]], "mg%d" % half)
                attn_head(h, hook=(lambda h=h: sel_finish(h + 1)) if h + 1 < 8 else None)
            prefetch("u0", wA[32])
            prefetch("zg0", wA[48])
            T.barrier()
            chk("B1")

            wvg_sb = mgB
            vn_sb = L(0, 8192).rearrange("p (t c) -> p t c", t=8)
            VNb = [Buf("vn%d" % i) for i in range(8)]
            gv = L(16384, 1024, F32)
            vn0 = L(20480, 1024, F32)
            lnvg_b = L(24576, 1024, F32)
            lnvb_b = L(28672, 1024, F32)
            bsb = L(32768, 1024, F32)
            gu = [L(36864 + i * 2048, 1024) for i in range(2)]
            szg = [L(40960 + i * 2048, 1024) for i in range(2)]
            wst = L(45056, 1024, F32).rearrange("p (g t) -> p g t", g=8)
            bst = sbt("bst", [128, 4, 6], F32)
            mvs = sbt("mvs", [128, 2], F32)
            rstd = sbt("rstd", [128, 1], F32)
            trilf = sbt("trilf", [128, 128], F32)
            mtmp = sbt("mtmp", [128, 512], F32)
            gvB, vn0B, c2B, bstB, mvsB, rstdB, mtmpB = [Buf(n) for n in "gv vn0 c2 bst mvs rstd mtmp".split()]
            GUb = [Buf("gu%d" % i) for i in range(2)]
            SGb = [Buf("szg%d" % i) for i in range(2)]
            T.dma("sp", lnvg_b, lnvg_d.partition_broadcast(128), [], [c2B], "c2")
            T.dma("sp", lnvb_b, lnvb_d.partition_broadcast(128), [], [c2B], "c2")
            T.dma("sp", bsb, bs_d.partition_broadcast(128), [], [c2B], "c2")
            T.dma("sp", wst, wsT_d, [], [c2B], "c2")
            T.dma("sp", trilf[:], tril_d, [], [c2B], "c2")
            T.op("dve", lambda e: e.tensor_tensor(out=wcb[:], in0=wst, in1=trilf[:].unsqueeze(1).broadcast_to([128, 8, 128]), op=ALU.mult), [c2B], [c2B])
            rotB = Rot([0, 1, 2, 3])
            for tt in range(8):
                for half in range(2):
                    ps, pb = rotB.next()
                    for kc in range(16):
                        mm(ps[:, :], xT_own[:, kc, tt * 128:(tt + 1) * 128], wvg_sb[:, half, kc, :], kc == 0, kc == 15,
                           [MGb[half], XTb[tt // 2]], [pb])
                    evac(gv[:, half * 512:(half + 1) * 512], ps[:, :], [pb], [gvB], func=AF.Gelu_apprx_tanh)
                for i in range(2):
                    T.op("dve", lambda e, i=i: e.bn_stats(out=bst[:, i, :], in_=gv[:, i * 512:(i + 1) * 512]), [gvB], [bstB])
                T.op("dve", lambda e: e.bn_aggr(out=mvs[:], in_=bst[:, 0:2, :]), [bstB], [mvsB])
                T.op("act", lambda e: e.activation(out=rstd[:], in_=mvs[:, 1:2], func=AF.Sqrt, bias=LN_EPS, scale=1.0), [mvsB], [rstdB])
                T.op("dve", lambda e: e.reciprocal(out=rstd[:], in_=rstd[:]), [rstdB], [rstdB])
                T.op("dve", lambda e: e.tensor_scalar(out=vn0, in0=gv, scalar1=mvs[:, 0:1], scalar2=rstd[:, 0:1], op0=ALU.subtract, op1=ALU.mult),
                     [gvB, mvsB, rstdB], [vn0B])
                T.op("dve", lambda e: e.tensor_tensor(out=vn0, in0=vn0, in1=lnvg_b, op=ALU.mult), [vn0B, c2B], [vn0B])
                T.op("dve", lambda e, tt=tt: e.tensor_tensor(out=vn_sb[:, tt, :], in0=vn0, in1=lnvb_b, op=ALU.add), [vn0B, c2B], [VNb[tt]])

            def proj_g(g):
                proj_fm(wA[32 + g], gu[g % 2], GUb[g % 2], func=AF.Gelu_apprx_tanh, rot=rotB, key="u%d" % g)
                proj_fm(wA[48 + g], szg[g % 2], SGb[g % 2], func=AF.Silu, rot=rotB, key="zg%d" % g)

            proj_g(0)
            for g in range(8):
                if g + 1 < 8:
                    proj_g(g + 1)
                if g == 4:
                    for half in range(2):
                        for kh in range(2):
                            T.dma("pool", mgB[:, half, kh * 8:(kh + 1) * 8, :], wmv[half][:, kh * 8:(kh + 1) * 8, :], [], [MGb[half]], "mg%d" % half)
                for hb in range(2):
                    psM, psMb = PS[4 + hb], Pb[4 + hb]
                    for t4 in range(4):
                        tt = hb * 4 + t4
                        mm(psM[:, t4 * 128:(t4 + 1) * 128], vn_sb[:, tt, g * 128:(g + 1) * 128], wcb[:, g, :], True, True, [VNb[tt], c2B], [psMb])
                    T.op("dve", lambda e, psM=psM, g=g: e.tensor_tensor(out=mtmp[:].rearrange("p (a b) -> p a b", a=4), in0=psM[:, :].rearrange("p (a b) -> p a b", a=4),
                                                                      in1=bsb[:, g * 128:(g + 1) * 128].unsqueeze(1).broadcast_to([128, 4, 128]), op=ALU.add),
                         [psMb, c2B], [mtmpB])
                    T.op("dve", lambda e, g=g, hb=hb: e.tensor_tensor(out=mtmp[:], in0=mtmp[:], in1=gu[g % 2][:, hb * 512:(hb + 1) * 512], op=ALU.mult), [mtmpB, GUb[g % 2]], [mtmpB])
                    T.op("dve", lambda e, g=g, hb=hb: e.tensor_tensor(out=Yv[:, 8 + g, hb * 512:(hb + 1) * 512], in0=mtmp[:], in1=szg[g % 2][:, hb * 512:(hb + 1) * 512], op=ALU.mult),
                         [mtmpB, SGb[g % 2]], [Yb[8 + g]])
            prefetch("mk0", wmk[0])
            prefetch("mk1", wmk[1])
            T.barrier()
            chk("B2")

            memT = L(0, 4096).rearrange("p (k m) -> p k m", k=16)
            mkT = L(8192, 2048).rearrange("p (c m) -> p c m", c=8)
            mvv = L(12288, 2048).rearrange("p (t c) -> p t c", t=2)
            qc = [L(16384 + i * 4096, 2048).rearrange("p (c t) -> p c t", c=2) for i in range(2)]
            szc = [L(24576 + i * 4096, 2048).rearrange("p (c t) -> p c t", c=2) for i in range(2)]
            ptc = [L(32768 + i * 1024, 512) for i in range(3)]
            rdc = L(36864, 512, F32)
            otc = L(38912, 512, F32)
            otc2 = [otc, L(40960, 512, F32)]
            otc2B = [Buf("otc0"), Buf("otc1")]
            memB, mkB, mvB, rdcB, otcB = [Buf(n) for n in "memT mkT mvv rdc otc".split()]
            QCb = [Buf("qc%d" % i) for i in range(2)]
            SCb = [Buf("szc%d" % i) for i in range(2)]
            PCb = [Buf("ptc%d" % i) for i in range(3)]
            T.dma("pool", memT, memTl, [], [memB], "memT")
            rotC = Rot([0, 1, 2, 3])
            for cg in range(8):
                w, wB = stream(wmk[cg], "mk%d" % cg)
                ps, pb = rotC.next()
                for kc in range(16):
                    mm(ps[:, 0:256], w[:, kc, :], memT[:, kc, :], kc == 0, kc == 15, [wB, memB], [pb])
                evac(mkT[:, cg, :], ps[:, 0:256], [pb], [mkB])
            for mt in range(2):
                for half in range(2):
                    ps, pb = rotC.next()
                    for kc in range(16):
                        mm(ps[:, :], memT[:, kc, mt * 128:(mt + 1) * 128], mgB[:, half, kc, :], kc == 0, kc == 15, [memB, MGb[half]], [pb])
                    evac(mvv[:, mt, half * 512:(half + 1) * 512], ps[:, :], [pb], [mvB])

            def proj_c(hc):
                for dc in range(2):
                    proj_fm(wA[56 + 2 * hc + dc], qc[hc % 2][:, dc, :], QCb[hc % 2], rot=rotC)
                    proj_fm(wA[64 + 2 * hc + dc], szc[hc % 2][:, dc, :], SCb[hc % 2], func=AF.Silu, rot=rotC)

            proj_c(0)
            pcq = [0]
            pendc = []

            def score_c(hc, half, mt):
                hs = slice(half * 512, (half + 1) * 512)
                ps, pb = rotC.next()
                for dc in range(2):
                    mm(ps[:, :], mkT[:, 2 * hc + dc, mt * 128:(mt + 1) * 128], qc[hc % 2][:, dc, hs], dc == 0, dc == 1, [mkB, QCb[hc % 2]], [pb])
                k = pcq[0] % 3
                pcq[0] += 1
                T.op("act", lambda e, ps=ps, k=k: e.activation(out=ptc[k], in_=ps[:, :], func=AF.Exp, scale=SCALE_C), [pb], [PCb[k]])
                return (hc, half, mt, k)

            def pv_c(hc, half, mt, k):
                hs = slice(half * 512, (half + 1) * 512)
                for dc in range(2):
                    mm(PS[4 + dc][:, :], mvv[:, mt, (2 * hc + dc) * 128:(2 * hc + dc + 1) * 128], ptc[k], mt == 0, mt == 1, [mvB, PCb[k]], [Pb[4 + dc]])
                mm(PS[6][:, :], onesb[:], ptc[k], mt == 0, mt == 1, [cB, PCb[k]], [Pb[6]])
                if mt == 1:
                    T.op("act", lambda e: e.copy(out=rdc, in_=PS[6][:, :]), [Pb[6]], [rdcB])
                    for dc in range(2):
                        T.op("act", lambda e, dc=dc: e.copy(out=otc2[dc], in_=PS[4 + dc][:, :]), [Pb[4 + dc]], [otc2B[dc]])
                    T.op("dve", lambda e: e.reciprocal(out=rdc, in_=rdc), [rdcB], [rdcB])
                    for dc in range(2):
                        T.op("dve", lambda e, dc=dc: e.tensor_tensor(out=otc2[dc], in0=otc2[dc], in1=rdc, op=ALU.mult), [otc2B[dc], rdcB], [otc2B[dc]])
                        T.op("dve", lambda e, dc=dc, hc=hc, hs=hs: e.tensor_tensor(out=Yv[:, 16 + 2 * hc + dc, hs], in0=otc2[dc], in1=szc[hc % 2][:, dc, hs], op=ALU.mult),
                             [otc2B[dc], SCb[hc % 2]], [Yb[16 + 2 * hc + dc]])

            for hc in range(4):
                if hc + 1 < 4:
                    proj_c(hc + 1)
                for half in range(2):
                    for mt in range(2):
                        pendc.append(score_c(hc, half, mt))
                        if len(pendc) > 2:
                            pv_c(*pendc.pop(0))
                while pendc:
                    pv_c(*pendc.pop(0))
            prefetch("g0_0", wA[72])
            prefetch("g1_0", wA[88])
            T.barrier()
            chk("B3")

            WQ = [R_LW[:, 16384:24576].rearrange("p (k n) -> p k n", k=16),
                  R_LW[:, 0:8192].rearrange("p (k n) -> p k n", k=16),
                  R_LW[:, 8192:16384].rearrange("p (k n) -> p k n", k=16)]
            WQb = [Buf("wq%d" % i) for i in range(3)]
            gate = [L(i * 4096, 1024, F32) for i in range(3)]
            GTb = [Buf("gate%d" % i) for i in range(3)]
            wbr_sb = [[L(12288 + (p * 3 + br) * 2048, 1024).rearrange("p (k n) -> p k n", k=8) for br in range(3)] for p in range(2)]
            WBRb = [[Buf("wbr%d_%d" % (p, br)) for br in range(3)] for p in range(2)]
            m0 = L(24576, 512, F32)
            m1 = L(26624, 512, F32)
            m0B, m1B = Buf("m0"), Buf("m1")
            rotG = Rot([0, 1, 2, 3])
            for fc in range(16):
                par = fc % 2
                if fc == 6:
                    for kh in range(2):
                        T.dma("pool", WQ[0][:, kh * 8:(kh + 1) * 8, :], wo[0][:, kh * 8:(kh + 1) * 8, :], [], [WQb[0]], "wq0")
                for br in range(3):
                    T.dma("pool", wbr_sb[par][br], wbr[br, fc], [], [WBRb[par][br]], "wbr%d_%d" % (par, br))
                for br in range(3):
                    proj_fm(wA[72 + br * 16 + fc], gate[br], GTb[br], func=AF.Sigmoid, rot=rotG, key="g%d_%d" % (br, fc))
                for half in range(2):
                    hs = slice(half * 512, (half + 1) * 512)
                    for br in range(3):
                        for kc in range(8):
                            mm(PS[4 + br][:, :], wbr_sb[par][br][:, kc, :], Yv[:, br * 8 + kc, hs], kc == 0, kc == 7, [WBRb[par][br], Yb[br * 8 + kc]], [Pb[4 + br]])
                    T.op("dve", lambda e, hs=hs: e.tensor_tensor(out=m0, in0=PS[4][:, :], in1=gate[0][:, hs], op=ALU.mult), [Pb[4], GTb[0]], [m0B])
                    T.op("dve", lambda e, hs=hs: e.tensor_tensor(out=m1, in0=PS[5][:, :], in1=gate[1][:, hs], op=ALU.mult), [Pb[5], GTb[1]], [m1B])
                    T.op("dve", lambda e: e.tensor_tensor(out=m0, in0=m0, in1=m1, op=ALU.add), [m0B, m1B], [m0B])
                    T.op("dve", lambda e, hs=hs: e.tensor_tensor(out=m1, in0=PS[6][:, :], in1=gate[2][:, hs], op=ALU.mult), [Pb[6], GTb[2]], [m1B])
                    T.op("dve", lambda e, fc=fc, hs=hs: e.tensor_tensor(out=merged[:, fc, hs], in0=m0, in1=m1, op=ALU.add), [m0B, m1B], [MRb[fc]])
            T.barrier()
            chk("C")

            for q4 in (1, 2):
                for kh in range(2):
                    T.dma("pool", WQ[q4][:, kh * 8:(kh + 1) * 8, :], wo[q4][:, kh * 8:(kh + 1) * 8, :], [], [WQb[q4]], "wq%d" % q4)
            lng_b = R_LW[:, 24576:28672].bitcast(F32)
            lnb_b = R_LW[:, 28672:32768].bitcast(F32)
            c3B = Buf("c3")
            T.dma("sp", lng_b, lng_d.partition_broadcast(128), [], [c3B], "c3")
            T.dma("sp", lnb_b, lnb_d.partition_broadcast(128), [], [c3B], "c3")
            rr = [R_XT[:, i * 4096:(i + 1) * 4096].bitcast(F32) for i in range(4)] + \
                 [R_Y[:, i * 4096:(i + 1) * 4096].bitcast(F32) for i in range(4)]
            RRb = [Buf("rr%d" % i) for i in range(8)]
            xt = [R_Y[:, 16384 + i * 4096: 16384 + (i + 1) * 4096].bitcast(F32) for i in range(2)]
            XIb = [Buf("xi%d" % i) for i in range(2)]
            nbias = sbt("nbias", [128, 1], F32)
            nbB = Buf("nbias")
            outB = Buf("out")
            rotD = Rot([0, 1, 2, 3, 4, 5, 6])

            def out_q(tt, q4):
                ps, pb = rotD.next()
                cs = slice(q4 * 512, (q4 + 1) * 512)
                wq, wqB = (WQ[q4], WQb[q4]) if q4 < 3 else (WQ[0], WQb[0])
                for kc in range(16):
                    mm(ps[:, :], merged[:, kc, tt * 128:(tt + 1) * 128], wq[:, kc, :], kc == 0, kc == 15, [MRb[kc], wqB], [pb])
                T.op("act", lambda e, ps=ps, tt=tt, cs=cs: e.copy(out=rr[tt][:, cs], in_=ps[:, :]), [pb], [RRb[tt]])

            for q4 in (0, 1):
                for tt in range(8):
                    out_q(tt, q4)
            for kh in range(2):
                T.dma("pool", WQ[0][:, kh * 8:(kh + 1) * 8, :], wo[3][:, kh * 8:(kh + 1) * 8, :], [], [WQb[0]], "wq0")
            bst2 = [sbt("bst2_%d" % i, [128, 4, 6], F32) for i in range(2)]
            mvs2 = [sbt("mvs2_%d" % i, [128, 2], F32) for i in range(2)]
            rstd2 = [sbt("rstd2_%d" % i, [128, 1], F32) for i in range(2)]
            nb2 = [sbt("nb2_%d" % i, [128, 1], F32) for i in range(2)]
            bst2B = [Buf("bst2_%d" % i) for i in range(2)]
            mvs2B = [Buf("mvs2_%d" % i) for i in range(2)]
            rstd2B = [Buf("rstd2_%d" % i) for i in range(2)]
            nb2B = [Buf("nb2_%d" % i) for i in range(2)]

            def stage1(tt):
                sl = tt % 2
                T.dma("sp", xt[sl], xown[tt], [], [XIb[sl]], "xi%d" % sl)
                for q4 in (2, 3):
                    out_q(tt, q4)

            def stage2(tt):
                sl = tt % 2
                T.op("dve", lambda e: e.scalar_tensor_tensor(out=rr[tt], in0=xt[sl], scalar=DN_ALPHA, in1=rr[tt], op0=ALU.mult, op1=ALU.add),
                     [RRb[tt], XIb[sl]], [RRb[tt]])
                for q4 in range(4):
                    T.op("dve", lambda e, q4=q4: e.bn_stats(out=bst2[sl][:, q4, :], in_=rr[tt][:, q4 * 512:(q4 + 1) * 512]), [RRb[tt]], [bst2B[sl]])
                T.op("dve", lambda e: e.bn_aggr(out=mvs2[sl][:], in_=bst2[sl][:, :, :]), [bst2B[sl]], [mvs2B[sl]])
                T.op("act", lambda e: e.activation(out=rstd2[sl][:], in_=mvs2[sl][:, 1:2], func=AF.Sqrt, bias=LN_EPS, scale=1.0), [mvs2B[sl]], [rstd2B[sl]])
                T.op("dve", lambda e: e.reciprocal(out=rstd2[sl][:], in_=rstd2[sl][:]), [rstd2B[sl]], [rstd2B[sl]])
                T.op("dve", lambda e: e.scalar_tensor_tensor(out=nb2[sl][:], in0=mvs2[sl][:, 0:1], scalar=-1.0, in1=rstd2[sl][:], op0=ALU.mult, op1=ALU.mult),
                     [mvs2B[sl], rstd2B[sl]], [nb2B[sl]])

            def stage3a(tt):
                sl = tt % 2
                T.op("act", lambda e: e.activation(out=rr[tt], in_=rr[tt], func=AF.Identity, bias=nb2[sl][:, 0:1], scale=rstd2[sl][:, 0:1]),
                     [RRb[tt], nb2B[sl], rstd2B[sl]], [RRb[tt]])
                T.op("pool", lambda e: e.tensor_tensor(out=rr[tt], in0=rr[tt], in1=lng_b, op=ALU.mult), [RRb[tt], c3B], [RRb[tt]])

            def stage3b(tt):
                T.op("dve", lambda e: e.tensor_tensor(out=rr[tt], in0=rr[tt], in1=lnb_b, op=ALU.add), [RRb[tt], c3B], [RRb[tt]])
                T.dma("sp", out_d[tt], rr[tt], [RRb[tt]], [outB], "o%d" % (tt % 2))

            stage1(0)
            for tt in range(8):
                if tt + 1 < 8:
                    stage1(tt + 1)
                stage2(tt)
                stage3a(tt)
                if tt >= 1:
                    stage3b(tt - 1)
            stage3b(7)
            T.op("sp", lambda e: None, [outB], [])
        try:
            _phases()
        except _Stop:
            import os
            if os.environ.get('NOBAR'):
                for e_ in ENGS:
                    T.op(e_, lambda e: None, [cB], [])
            else:
                T.barrier()
        T.emit(st)
    return nc


def _core_layout(c):
    b, j = c // 4, c % 4
    own = [j, 7 - j, 8 + j, 15 - j]
    others = [n for n in range(16) if n not in own]
    order = own + others
    return b, own, order


_CONST = {}


def _consts():
    if _CONST:
        return _CONST
    ident = np.eye(128, dtype=np.float32)
    esel = np.zeros((128, 16, 128), np.float32)
    for n in range(16):
        esel[n, n, :] = 1.0
    p = np.arange(128)[:, None, None]
    kt = np.arange(2)[None, :, None]
    f = np.arange(256)[None, None, :]
    cb = np.where(kt * 128 + p <= f, 0.0, NEG).astype(np.float32)
    s = np.arange(128)[:, None]
    t = np.arange(128)[None, :]
    tril = (s <= t).astype(np.float32)
    _CONST.update(ident=ident, esel=esel.reshape(128, 2048), cb=np.ascontiguousarray(cb), tril=tril)
    return _CONST


def _lay_A(w):
    K, C = w.shape
    return np.ascontiguousarray(w.reshape(K // 128, 128, C // 128, 128).transpose(2, 1, 0, 3))


def _lay_B(w):
    K, C = w.shape
    return np.ascontiguousarray(w.reshape(K // 128, 128, C // 512, 512).transpose(2, 1, 0, 3))


def make_in_maps(x, mem, w_in, w_mem_k, w_mem_v, w_s, b_s, ln_v_g, ln_v_b,
                 w_branch_attn, w_branch_sgu, w_branch_mem, w_out, ln_g, ln_b, cores=range(8)):
    f = lambda a: np.asarray(a, dtype=np.float32)
    x, mem, w_in = f(x), f(mem), f(w_in)
    shared = dict(_consts())
    shared["wA"] = _lay_A(w_in)
    shared["wBva"] = _lay_B(w_in[:, 2048:3072])
    shared["wBvg"] = _lay_B(w_in[:, 5120:6144])
    shared["wmk"] = _lay_A(f(w_mem_k))
    shared["wmv"] = _lay_B(f(w_mem_v))
    shared["wbr"] = np.stack([_lay_A(f(w)) for w in (w_branch_attn, w_branch_sgu, w_branch_mem)])
    shared["wo"] = _lay_B(f(w_out))
    shared["wsT"] = np.ascontiguousarray(f(w_s).transpose(2, 0, 1))
    shared["bs"] = np.ascontiguousarray(f(b_s).reshape(1, 1024))
    shared["lnvg"] = f(ln_v_g).reshape(1, 1024)
    shared["lnvb"] = f(ln_v_b).reshape(1, 1024)
    shared["lng"] = f(ln_g).reshape(1, 2048)
    shared["lnb"] = f(ln_b).reshape(1, 2048)
    maps = []
    for c in cores:
        b, own, order = _core_layout(c)
        perm = np.concatenate([np.arange(n * 256, (n + 1) * 256) for n in order])
        xp = x[b][perm]
        m = dict(shared)
        m["xTl"] = np.ascontiguousarray(xp.reshape(16, 256, 16, 128).transpose(0, 3, 2, 1))
        m["xown"] = np.ascontiguousarray(xp[:1024].reshape(8, 128, 2048))
        qblk = np.repeat(np.array(own), 256)
        past = (np.array(order)[None, :] < qblk[:, None]).astype(np.float32)
        m["past"] = np.ascontiguousarray(past.reshape(8, 128, 16).transpose(1, 0, 2).reshape(128, 128))
        m["memTl"] = np.ascontiguousarray(mem[b].T.reshape(16, 128, 256).transpose(1, 0, 2))
        maps.append(m)
    return maps


_NC = []


def kernel(**inputs):
    if not _NC:
        _NC.append(build_program())
    nc = _NC[0]
    in_maps = make_in_maps(**inputs)
    res = run_bass_kernel_spmd(nc, in_maps, core_ids=list(range(8)))
    out = np.zeros((2, 4096, 2048), np.float32)
    for c in range(8):
        b, own, order = _core_layout(c)
        o = np.asarray(res.results[c]["out"]).reshape(1024, 2048)
        for s, n in enumerate(own):
            out[b, n * 256:(n + 1) * 256] = o[s * 256:(s + 1) * 256]
    return out
```
